# Optimizing a Trainium2 kernel written in Bass

```python
import jax, jax.numpy as jnp
from jax import lax
import numpy as np

D_MODEL = 1024
BATCH = 4
SEQ = 4096
DEPTH = 2

GRID_W = 64
CTX_LEN = 256
D_A = D_MODEL
SGU_GROUPS = 8
SGU_GROUP_DIM = D_A // SGU_GROUPS
SGU_CHUNK = 128
D_B = D_MODEL
HGRN_HEAD_DIM = 128
HGRN_HEADS = D_B // HGRN_HEAD_DIM
HGRN_CHUNK = 64
D_FF = 2816
CONV_W = 3
N_MOD = 6
RMS_EPS = 1e-6
LN_EPS = 1e-5
IN_SPLITS = (D_B, 2 * D_B, 3 * D_B, 4 * D_B, 4 * D_B + D_A, 4 * D_B + 2 * D_A, 5 * D_B + 2 * D_A, 5 * D_B + 2 * D_A + D_MODEL)
D_IN = 5 * D_B + 2 * D_A + 2 * D_MODEL

kernel_name = 'hybrid_sgu_hgrn2_convffn_prefix_dit'


def rms_norm(x, w):
    xf = x.astype(jnp.float32)
    y = xf * lax.rsqrt(jnp.mean(xf * xf, axis=-1, keepdims=True) + RMS_EPS)
    return (y * w.astype(jnp.float32)).astype(x.dtype)


def layer_norm(x, w, b):
    xf = x.astype(jnp.float32)
    mu = jnp.mean(xf, axis=-1, keepdims=True)
    var = jnp.mean(jnp.square(xf - mu), axis=-1, keepdims=True)
    y = (xf - mu) * lax.rsqrt(var + LN_EPS)
    return (y * w.astype(jnp.float32) + b.astype(jnp.float32)).astype(x.dtype)


def modulate(h, shift, scale):
    return h * (1 + scale) + shift


def to_heads(t):
    bsz, length, _ = t.shape
    return t.reshape(bsz, length, HGRN_HEADS, HGRN_HEAD_DIM).transpose(0, 2, 1, 3)


def hgrn_forget(f_logit, lb):
    z = f_logit.astype(jnp.float32)
    f = lb + (1 - lb) * jax.nn.sigmoid(z)
    return to_heads((1 - lb) * jax.nn.sigmoid(-z)), to_heads(jnp.log(f))


def gla_chunked(q, k, v, g, s0):
    bsz, heads, length, _ = q.shape
    n = length // HGRN_CHUNK
    split = lambda t: t.reshape(bsz, heads, n, HGRN_CHUNK, t.shape[-1])
    q, k, v, g = split(q), split(k), split(v), split(g)
    b = jnp.cumsum(g, axis=3)
    b_last = b[:, :, :, -1:, :]
    ref = b[:, :, :, HGRN_CHUNK // 2 - 1:HGRN_CHUNK // 2, :]
    scores = jnp.einsum('bhntk,bhnsk->bhnts', q * jnp.exp(b - ref), k * jnp.exp(ref - b))
    lower = jnp.tril(jnp.ones((HGRN_CHUNK, HGRN_CHUNK), dtype=bool))
    o_intra = jnp.einsum('bhnts,bhnsv->bhntv', jnp.where(lower, scores, 0.0), v)
    q_inter = q * jnp.exp(b)
    kv = jnp.einsum('bhnsk,bhnsv->bhnkv', k * jnp.exp(b_last - b), v)
    decay = jnp.exp(b_last[:, :, :, 0, :])

    def step(state, xs):
        q_n, kv_n, d_n = xs
        o_n = jnp.einsum('bhtk,bhkv->bhtv', q_n, state)
        return d_n[..., None] * state + kv_n, o_n

    move = lambda t: jnp.moveaxis(t, 2, 0)
    s_final, o_inter = lax.scan(step, s0, (move(q_inter), move(kv), move(decay)))
    o = o_intra + jnp.moveaxis(o_inter, 0, 2)
    return o.reshape(bsz, heads, length, -1), s_final


def hgrn_bidir(q, f_fwd, f_bwd, i, lb_fwd, lb_bwd, s0_fwd, s0_bwd):
    qh = to_heads(jax.nn.silu(q.astype(jnp.float32)))
    ih = to_heads(i.astype(jnp.float32))
    k_f, g_f = hgrn_forget(f_fwd, lb_fwd)
    k_b, g_b = hgrn_forget(f_bwd, lb_bwd)
    o_f, s_f = gla_chunked(qh, k_f, ih, g_f, s0_fwd)
    rev = lambda t: jnp.flip(t, axis=2)
    o_b, s_b = gla_chunked(rev(qh), rev(k_b), rev(ih), rev(g_b), s0_bwd)
    return o_f + rev(o_b), s_f, s_b


def hgrn_readout(o, og, norm_w):
    o = o * lax.rsqrt(jnp.mean(o * o, axis=-1, keepdims=True) + RMS_EPS) * norm_w.astype(jnp.float32)
    bsz, _, length, _ = o.shape
    o = o.transpose(0, 2, 1, 3).reshape(bsz, length, D_B).astype(og.dtype)
    return o * jax.nn.silu(og)


def sgu(u, v, ln_w, ln_b, w_s, b_s):
    bsz, length, _ = v.shape
    vn = layer_norm(v, ln_w, ln_b).reshape(bsz, length // SGU_CHUNK, SGU_CHUNK, SGU_GROUPS, SGU_GROUP_DIM)
    mixed = jnp.einsum('gts,bnsgd->bntgd', w_s, vn) + b_s.T[:, :, None]
    return u * mixed.reshape(bsz, length, D_A)


def token_mixer_out(parts, o_b, sgu_ln_w, sgu_ln_b, sgu_w, sgu_b, hgrn_norm_w, w_a, w_b, w_o):
    u, v, og, gate_a, gate_b = parts
    y_a = sgu(jax.nn.gelu(u), jax.nn.gelu(v), sgu_ln_w, sgu_ln_b, sgu_w, sgu_b)
    y_b = hgrn_readout(o_b, og, hgrn_norm_w)
    merged = jax.nn.sigmoid(gate_a) * (y_a @ w_a) + jax.nn.sigmoid(gate_b) * (y_b @ w_b)
    return merged @ w_o


def dwconv_grid(a, conv_w, conv_b):
    bsz, length, ch = a.shape
    rows = length // GRID_W
    y = lax.conv_general_dilated(a.reshape(bsz, rows, GRID_W, ch), conv_w[:, :, None, :].astype(a.dtype),
                                 window_strides=(1, 1), padding='SAME',
                                 dimension_numbers=('NHWC', 'HWIO', 'NHWC'), feature_group_count=ch)
    return y.reshape(bsz, length, ch) + conv_b


def dwconv_seq(a, conv_w, conv_b):
    y = lax.conv_general_dilated(a, conv_w[CONV_W // 2][:, None, :].astype(a.dtype),
                                 window_strides=(1,), padding='SAME',
                                 dimension_numbers=('NWC', 'WIO', 'NWC'), feature_group_count=a.shape[-1])
    return y + conv_b


def conv_ffn(h, w_up, conv_w, conv_b, w_down, on_grid):
    a, v = jnp.split(h @ w_up, 2, axis=-1)
    a = dwconv_grid(a, conv_w, conv_b) if on_grid else dwconv_seq(a, conv_w, conv_b)
    return (jax.nn.gelu(a) * v) @ w_down


def setup_inputs(seed: int = 0) -> dict:
    key = jax.random.key(seed)
    ks = jax.random.split(key, 24)
    nrm = lambda k, shape, s: s * jax.random.normal(k, shape, jnp.float32)
    gain = lambda k, shape: 1.0 + nrm(k, shape, 0.02)
    return {
        'x': nrm(ks[0], (BATCH, SEQ, D_MODEL), 1.0),
        'c': nrm(ks[1], (BATCH, D_MODEL), 1.0),
        'ctx': nrm(ks[2], (BATCH, CTX_LEN, D_MODEL), 1.0),
        'c_ctx': nrm(ks[3], (D_MODEL,), 1.0),
        'ada_w': nrm(ks[4], (DEPTH, D_MODEL, N_MOD * D_MODEL), 0.5 * D_MODEL ** -0.5),
        'ada_b': nrm(ks[5], (DEPTH, N_MOD * D_MODEL), 0.02),
        'norm1_w': gain(ks[6], (DEPTH, D_MODEL)),
        'w_in': nrm(ks[7], (DEPTH, D_MODEL, D_IN), D_MODEL ** -0.5),
        'sgu_ln_w': gain(ks[8], (DEPTH, D_A)),
        'sgu_ln_b': nrm(ks[9], (DEPTH, D_A), 0.02),
        'sgu_w': nrm(ks[10], (DEPTH, SGU_GROUPS, SGU_CHUNK, SGU_CHUNK), SGU_CHUNK ** -0.5),
        'sgu_b': gain(ks[11], (DEPTH, SGU_GROUPS, SGU_CHUNK)),
        'hgrn_lower_bounds': nrm(ks[12], (DEPTH, 2 * D_B), 0.1),
        'hgrn_norm_w': gain(ks[13], (DEPTH, HGRN_HEAD_DIM)),
        'w_branch_a': nrm(ks[14], (DEPTH, D_A, D_MODEL), D_A ** -0.5),
        'w_branch_b': nrm(ks[15], (DEPTH, D_B, D_MODEL), D_B ** -0.5),
        'w_out': nrm(ks[16], (DEPTH, D_MODEL, D_MODEL), D_MODEL ** -0.5),
        'norm2_w': gain(ks[17], (DEPTH, D_MODEL)),
        'ffn_w_up': nrm(ks[18], (DEPTH, D_MODEL, 2 * D_FF), D_MODEL ** -0.5),
        'ffn_conv_w': nrm(ks[19], (DEPTH, CONV_W, CONV_W, D_FF), 1.0 / CONV_W),
        'ffn_conv_b': nrm(ks[20], (DEPTH, D_FF), 0.02),
        'ffn_w_down': nrm(ks[21], (DEPTH, D_FF, D_MODEL), D_FF ** -0.5),
        'final_norm_w': gain(ks[22], (D_MODEL,)),
    }


def reference(x, c, ctx, c_ctx, ada_w, ada_b, norm1_w, w_in, sgu_ln_w, sgu_ln_b, sgu_w, sgu_b,
              hgrn_lower_bounds, hgrn_norm_w, w_branch_a, w_branch_b, w_out, norm2_w,
              ffn_w_up, ffn_conv_w, ffn_conv_b, ffn_w_down, final_norm_w):
    lb = jax.nn.softmax(hgrn_lower_bounds.astype(jnp.float32), axis=0)
    lb = jnp.cumsum(lb, axis=0) - lb[0]
    zero_state = jnp.zeros((ctx.shape[0], HGRN_HEADS, HGRN_HEAD_DIM, HGRN_HEAD_DIM), jnp.float32)
    for l in range(DEPTH):
        last = l == DEPTH - 1
        mod_x = (jax.nn.silu(c) @ ada_w[l] + ada_b[l])[:, None, :]
        mod_c = jax.nn.silu(c_ctx) @ ada_w[l] + ada_b[l]
        sh1, sc1, g1, sh2, sc2, g2 = jnp.split(mod_x, N_MOD, axis=-1)
        csh1, csc1, cg1, csh2, csc2, cg2 = jnp.split(mod_c, N_MOD, axis=-1)
        lb_f, lb_b = lb[l, :D_B], lb[l, D_B:]

        h_c = modulate(rms_norm(ctx, norm1_w[l]), csh1, csc1)
        n_cols = IN_SPLITS[3] if last else D_IN
        parts_c = jnp.split(h_c @ w_in[l, :, :n_cols], IN_SPLITS[:3] if last else IN_SPLITS, axis=-1)
        o_c, s_f, s_b = hgrn_bidir(*parts_c[:4], lb_f, lb_b, zero_state, zero_state)

        h_x = modulate(rms_norm(x, norm1_w[l]), sh1, sc1)
        parts_x = jnp.split(h_x @ w_in[l], IN_SPLITS, axis=-1)
        o_x, _, _ = hgrn_bidir(*parts_x[:4], lb_f, lb_b, s_f, s_b)
        x = x + g1 * token_mixer_out(parts_x[4:], o_x, sgu_ln_w[l], sgu_ln_b[l], sgu_w[l], sgu_b[l],
                                     hgrn_norm_w[l], w_branch_a[l], w_branch_b[l], w_out[l])
        h2 = modulate(rms_norm(x, norm2_w[l]), sh2, sc2)
        x = x + g2 * conv_ffn(h2, ffn_w_up[l], ffn_conv_w[l], ffn_conv_b[l], ffn_w_down[l], True)

        if not last:
            ctx = ctx + cg1 * token_mixer_out(parts_c[4:], o_c, sgu_ln_w[l], sgu_ln_b[l], sgu_w[l], sgu_b[l],
                                              hgrn_norm_w[l], w_branch_a[l], w_branch_b[l], w_out[l])
            hc2 = modulate(rms_norm(ctx, norm2_w[l]), csh2, csc2)
            ctx = ctx + cg2 * conv_ffn(hc2, ffn_w_up[l], ffn_conv_w[l], ffn_conv_b[l], ffn_w_down[l], False)
    return rms_norm(x, final_norm_w)
```

```python
import numpy as np
from contextlib import ExitStack
import concourse.bass as bass
import concourse.mybir as mybir
from concourse.bass_utils import run_bass_kernel_spmd

F32 = mybir.dt.float32
BF16 = mybir.dt.bfloat16
AF = mybir.ActivationFunctionType
ALU = mybir.AluOpType
AX = mybir.AxisListType

ENGS = ("pe", "act", "dve", "pool", "sp")
D = 1024
SEQ = 4096
CTX = 256
NBX = SEQ // 128
NBC = CTX // 128
DFF = 2816
NCC = DFF // 128
DIN = 9216
EPS = 1e-6
LN_EPS = 1e-5


class V:
    __slots__ = ("ap", "keys")

    def __init__(self, ap, keys):
        self.ap = ap
        self.keys = keys


class Buf:
    def __init__(self, t, name):
        self.t = t
        self.name = name

    def __getitem__(self, idx):
        return V(self.t[idx], (self.name,))

    def k(self, sub, idx=slice(None)):
        return V(self.t[idx], ((self.name, sub),))

    def v(self, ap, sub=None):
        return V(ap, ((self.name, sub),) if sub is not None else (self.name,))


class Op:
    __slots__ = ("eng", "fn", "reads", "writes", "dma", "deps", "ticket", "sem", "has_dep", "idx", "dq", "bar")

    def __init__(self, eng, fn, reads, writes, dma, bar=False):
        self.eng = eng
        self.fn = fn
        self.reads = reads
        self.writes = writes
        self.dma = dma
        self.deps = ()
        self.ticket = None
        self.sem = None
        self.has_dep = False
        self.bar = bar


class Prog:
    NDMA_SEM = 12

    def __init__(self, nc):
        self.nc = nc
        self.ops = []
        self.uid = 0

    def sbuf_at(self, name, shape, dtype, offset):
        self.uid += 1
        nm = "%s_%d" % (name, self.uid)
        return Buf(self.nc.alloc_sbuf_tensor_at(nm, list(shape), dtype, offset=offset), nm)

    def psum(self, name, shape, dtype=F32):
        return Buf(self.nc.alloc_psum_tensor(name, list(shape), dtype), name)

    def dram(self, name, shape, dtype, kind="Internal"):
        return Buf(self.nc.dram_tensor(name, list(shape), dtype, kind=kind), name)

    def add(self, eng, fn, reads, writes, dma=False, bar=False):
        rk = []
        for r in reads:
            if isinstance(r, V):
                rk.extend(r.keys)
        wk = []
        for w in writes:
            if isinstance(w, V):
                wk.extend(w.keys)
        op = Op(eng, fn, tuple(rk), tuple(wk), dma, bar)
        self.ops.append(op)
        return op

    @staticmethod
    def _a(x):
        return x.ap if isinstance(x, V) else x

    def mm(self, out, lhsT, rhs, start=True, stop=True):
        a = self._a
        return self.add("pe", lambda e: e.matmul(a(out), a(lhsT), a(rhs), start=start, stop=stop),
                        [lhsT, rhs], [out])

    def tr(self, out, in_, ident):
        a = self._a
        return self.add("pe", lambda e: e.transpose(a(out), a(in_), a(ident)), [in_, ident], [out])

    def act(self, out, in_, func, bias=0.0, scale=1.0, accum_out=None):
        a = self._a
        kw = {}
        if accum_out is not None:
            kw["accum_out"] = a(accum_out)
        w = [out] + ([accum_out] if accum_out is not None else [])
        return self.add("act", lambda e: e.activation(a(out), a(in_), func, bias=a(bias), scale=a(scale), **kw),
                        [in_, bias, scale], w)

    def tt(self, out, in0, in1, op, eng="dve"):
        a = self._a
        return self.add(eng, lambda e: e.tensor_tensor(a(out), a(in0), a(in1), op), [in0, in1], [out])

    def ts(self, out, in0, s1, op0, s2=None, op1=None, eng="dve"):
        a = self._a
        if op1 is None:
            return self.add(eng, lambda e: e.tensor_scalar(a(out), a(in0), a(s1), None, op0), [in0, s1], [out])
        return self.add(eng, lambda e: e.tensor_scalar(a(out), a(in0), a(s1), a(s2), op0, op1),
                        [in0, s1, s2], [out])

    def stt(self, out, in0, scalar, in1, op0, op1, eng="dve"):
        a = self._a
        return self.add(eng, lambda e: e.scalar_tensor_tensor(a(out), a(in0), a(scalar), a(in1), op0, op1),
                        [in0, scalar, in1], [out])

    def copy(self, out, in_, eng="dve"):
        a = self._a
        if eng == "act":
            return self.add(eng, lambda e: e.activation(a(out), a(in_), AF.Copy), [in_], [out])
        return self.add(eng, lambda e: e.tensor_copy(a(out), a(in_)), [in_], [out])

    def cpred(self, out, mask, data):
        a = self._a
        return self.add("dve", lambda e: e.copy_predicated(a(out), a(mask), a(data)), [mask, data, out], [out])

    def memset(self, out, val, eng="pool"):
        a = self._a
        return self.add(eng, lambda e: e.memset(a(out), val), [], [out])

    def recip(self, out, in_):
        a = self._a
        return self.add("dve", lambda e: e.reciprocal(a(out), a(in_)), [in_], [out])

    def dma(self, out, in_, eng="sp", **kw):
        a = self._a
        return self.add(eng, lambda e: e.dma_start(out=a(out), in_=a(in_), **kw), [in_], [out], dma=True)

    def barrier(self):
        for e in ENGS:
            self.add(e, lambda eng: eng.nop(), [], [], bar=True)

    def resolve(self):
        last_w = {}
        readers = {}
        ops = self.ops
        last_on = {}
        dmas = []
        for i, op in enumerate(ops):
            op.idx = i
            deps = set()
            if op.bar:
                deps.update(last_on.values())
                deps.update(dmas)
            for k in op.reads:
                if k in last_w:
                    deps.add(last_w[k])
            for k in op.writes:
                if k in last_w:
                    deps.add(last_w[k])
                r = readers.get(k)
                if r:
                    deps.update(r.values())
            for k in op.reads:
                r = readers.setdefault(k, {})
                if op.dma:
                    r[("dma", i)] = i
                else:
                    r[op.eng] = i
            for k in op.writes:
                last_w[k] = i
                readers[k] = {}
            deps.discard(i)
            if op.eng == "pe" and not op.bar:
                deps = {d for d in deps if not (ops[d].eng == "pe" and not ops[d].dma)}
            op.deps = deps
            for d in deps:
                ops[d].has_dep = True
            if op.dma:
                if op.eng != "pool":
                    dmas.append(i)
            else:
                last_on[op.eng] = i
            if op.bar and op.eng == ENGS[-1]:
                dmas = []
        cnt = {e: 0 for e in ENGS}
        dcnt = {}
        for op in ops:
            if op.dma:
                n = dcnt.get(op.eng, 0)
                dcnt[op.eng] = n + 1
                op.dq = n
            elif op.has_dep:
                cnt[op.eng] += 1
                op.ticket = cnt[op.eng]

    def emit(self):
        nc = self.nc
        self.resolve()
        ops = self.ops
        with ExitStack() as st:
            esem = {e: st.enter_context(nc.semaphore("s_" + e)) for e in ENGS}
            dq_engs = sorted({op.eng for op in ops if op.dma})
            dsem = {e: [st.enter_context(nc.semaphore("d_%s_%d" % (e, i))) for i in range(self.NDMA_SEM)]
                    for e in dq_engs}
            K = self.NDMA_SEM
            for op in ops:
                if op.dma:
                    op.sem = dsem[op.eng][op.dq % K]
                    op.ticket = 16 * (op.dq // K + 1)
                elif op.ticket is not None:
                    op.sem = esem[op.eng]
            per_eng = {e: [op for op in ops if op.eng == e] for e in ENGS}
            block = st.enter_context(nc.Block())

            def run(engname, e):
                seen = {}
                for op in per_eng[engname]:
                    waits = {}
                    for d in op.deps:
                        dop = ops[d]
                        key = id(dop.sem)
                        if seen.get(key, 0) >= dop.ticket:
                            continue
                        if key not in waits or waits[key][1] < dop.ticket:
                            waits[key] = (dop.sem, dop.ticket)
                    if op.dma and op.ticket > 16:
                        key = id(op.sem)
                        prev = op.ticket - 16
                        if seen.get(key, 0) < prev and (key not in waits or waits[key][1] < prev):
                            waits[key] = (op.sem, prev)
                    for key, (s, tk) in waits.items():
                        e.wait_ge(s, tk)
                        seen[key] = tk
                    ins = op.fn(e)
                    if op.dma:
                        ins.then_inc(op.sem, 16)
                    elif op.ticket is not None:
                        ins.then_inc(op.sem, 1)

            @block.sync
            def _(e):
                run("sp", e)

            @block.scalar
            def _(e):
                run("act", e)

            @block.vector
            def _(e):
                run("dve", e)

            @block.gpsimd
            def _(e):
                run("pool", e)

            @block.tensor
            def _(e):
                run("pe", e)


class Arena:
    def __init__(self, P, base, limit=229312):
        self.P = P
        self.base = base
        self.cur = base
        self.limit = limit

    def reset(self):
        self.cur = self.base

    def alloc(self, name, shape, dtype):
        n = 1
        for s in shape[1:]:
            n *= s
        nbytes = n * (2 if dtype == BF16 else 4)
        nbytes = (nbytes + 63) // 64 * 64
        off = self.cur
        self.cur += nbytes
        assert self.cur <= self.limit, ("SBUF overflow", name, self.cur)
        return self.P.sbuf_at(name, shape, dtype, off)

    def ring(self, name, n, shape, dtype):
        return [self.alloc("%s%d" % (name, i), shape, dtype) for i in range(n)]


C_ID = 0
C_MDA = (128, 256)
C_MASK = (384, 512)
C_SEL = (640, 644)
C_ONE = 648
NCONST = 776


def make_consts():
    c = np.zeros((128, NCONST), np.float32)
    c[:, C_ID:C_ID + 128] = np.eye(128, dtype=np.float32)
    s = np.arange(128)[:, None]
    t = np.arange(128)[None, :]
    same = (s // 64) == (t // 64)
    js, jt = s % 64, t % 64
    c[:, C_MDA[0]:C_MDA[0] + 128] = same * ((js <= jt).astype(np.float32) - (js <= 31).astype(np.float32))
    c[:, C_MDA[1]:C_MDA[1] + 128] = same * ((js >= jt).astype(np.float32) - (js >= 32).astype(np.float32))
    c[:, C_MASK[0]:C_MASK[0] + 128] = (same & (js <= jt)).astype(np.float32)
    c[:, C_MASK[1]:C_MASK[1] + 128] = (same & (js >= jt)).astype(np.float32)
    sv = np.arange(128)
    ch, j = sv // 64, sv % 64
    for cc in range(2):
        c[:, C_SEL[0] + cc] = (ch == cc) & (j <= 31)
        c[:, C_SEL[0] + 2 + cc] = (ch == cc) & (j >= 32)
        c[:, C_SEL[1] + cc] = (ch == cc) & (j >= 32)
        c[:, C_SEL[1] + 2 + cc] = (ch == cc) & (j <= 31)
    c[:, C_ONE:C_ONE + 128] = 1.0
    return c


def build(stop_after=None, dbg=()):
    nc = bass.Bass("TRN2", target_bir_lowering=False)
    P = Prog(nc)
    I = {}

    def din(name, shape):
        I[name] = P.dram(name, shape, F32, kind="ExternalInput")
        return I[name]

    x_in = din("x", [SEQ, D])
    ctx_in = din("ctx", [CTX, D])
    cc_in = din("cc", [128, 8, 2])
    ada_w = din("ada_w", [2, D, 6 * D])
    ada_b = din("ada_b", [2, 6 * D])
    norm1_w = din("norm1_w", [2, D])
    w_in = din("w_in", [2, D, DIN])
    sgu_ln_w = din("sgu_ln_w", [2, D])
    sgu_ln_b = din("sgu_ln_b", [2, D])
    sgu_wT = din("sgu_wT", [2, 128, 8, 128])
    sgu_b = din("sgu_b", [2, D])
    lbraw = din("hgrn_lower_bounds", [2, 2 * D])
    hnw = din("hgrn_norm_w", [2, 128])
    w_a = din("w_branch_a", [2, D, D])
    w_b = din("w_branch_b", [2, D, D])
    w_o = din("w_out", [2, D, D])
    norm2_w = din("norm2_w", [2, D])
    w_up = din("ffn_w_up", [2, D, 2 * DFF])
    convw = din("convw", [2, 128, NCC, 9])
    convb = din("convb", [2, 128, NCC])
    w_dn = din("ffn_w_down", [2, DFF, D])
    fnw = din("final_norm_w", [1, D])
    consts_in = din("consts", [128, NCONST])
    out = P.dram("out", [SEQ, D], F32, kind="ExternalOutput")

    dbgk = set(dbg)

    def scratch(name, shape, dtype):
        return P.dram(name, shape, dtype, kind="ExternalOutput" if name in dbgk else "Internal")

    WB_in = [scratch("WB_in%d" % l, [D, DIN], BF16) for l in range(2)]
    WB_a = [scratch("WB_a%d" % l, [D, D], BF16) for l in range(2)]
    WB_b = [scratch("WB_b%d" % l, [D, D], BF16) for l in range(2)]
    WB_o = [scratch("WB_o%d" % l, [D, D], BF16) for l in range(2)]
    WB_up = [scratch("WB_up%d" % l, [D, 2 * DFF], BF16) for l in range(2)]
    WB_dn = [scratch("WB_dn%d" % l, [DFF, D], BF16) for l in range(2)]
    modD = scratch("modD", [2, 2, 6 * D], F32)
    NB = {"x": NBX, "c": NBC}
    XA = {"x": x_in, "c": ctx_in}
    HT = {s: scratch("HT_" + s, [NB[s], 128, D], BF16) for s in "xc"}
    OD = {(s, d): scratch("O%s_%s" % (d, s), [NB[s], 128, D], F32) for s in "xc" for d in "fb"}
    MB = {s: scratch("MB_" + s, [NB[s], 128, D], F32) for s in "xc"}
    XMID = {s: scratch("XMID_" + s, [NB[s] * 128, D], F32) for s in "xc"}
    GA = {"x": scratch("GA_x", [8, 128, NCC, 512], BF16), "c": scratch("GA_c", [1, 128, NCC, 256], BF16)}
    XN = {"x": scratch("XN_x", [SEQ, D], F32), "c": scratch("XN_c", [CTX, D], F32)}

    PS = [P.psum("ps%d" % i, [128, 1024], F32) for i in range(4)]

    def psf(i, h=None):
        if h is None:
            return V(PS[i].t[:, :], ((PS[i].name, 0), (PS[i].name, 1)))
        return V(PS[i].t[:, h * 512:(h + 1) * 512], ((PS[i].name, h),))

    def psb(i, h):
        return V(PS[i].t[:, h * 512:(h + 1) * 512].bitcast(BF16).rearrange("p (c t) -> p c t", t=128),
                 ((PS[i].name, h),))

    def ps3(i):
        return V(PS[i].t[:, :].rearrange("p (c t) -> p c t", t=128), ((PS[i].name, 0), (PS[i].name, 1)))

    A0 = Arena(P, 16512)
    consts = A0.alloc("consts", [128, NCONST], F32)
    identb = A0.alloc("identb", [128, 128], BF16)
    A = Arena(P, A0.cur)

    def cst(off, n=128):
        return consts[:, off:off + n]

    P.dma(consts[:, :], consts_in[:, :])
    P.copy(identb[:, :], cst(C_ID), eng="dve")

    def cast_w(dst, src, rows, cols):
        for r0 in range(0, rows, 1024):
            r1 = min(rows, r0 + 1024)
            for c0 in range(0, cols, 1024):
                c1 = min(cols, c0 + 1024)
                P.dma(V(dst.t[r0:r1, c0:c1], ((dst.name, r0 // 1024, c0 // 1024),)), V(src[r0:r1, c0:c1], ()), eng="pool")

    for l in range(2):
        cast_w(WB_in[l], w_in.t[l], D, DIN)
        cast_w(WB_a[l], w_a.t[l], D, D)
        cast_w(WB_b[l], w_b.t[l], D, D)
        cast_w(WB_o[l], w_o.t[l], D, D)
        cast_w(WB_up[l], w_up.t[l], D, 2 * DFF)
        cast_w(WB_dn[l], w_dn.t[l], DFF, D)

    def wkeys(WB, c0, ncol, rows=D):
        return tuple((WB.name, r, c) for r in range((rows + 1023) // 1024) for c in range(c0 // 1024, (c0 + ncol - 1) // 1024 + 1))

    def load_w(dst, WB, c0, ncol, rows=D):
        src = WB.t[:, c0:c0 + ncol].rearrange("(c p) j -> p c j", p=128)
        P.dma(dst[:, :, :], V(src, wkeys(WB, c0, ncol, rows)))

    def bc_load(dst, src_ap):
        P.dma(dst, V(src_ap.partition_broadcast(128), ()))

    def phase_mod():
        A.reset()
        ccs = A.alloc("ccs", [128, 8, 2], F32)
        scc = A.alloc("scc", [128, 8, 2], F32)
        adab = A.alloc("adab", [1, 12 * D], F32)
        aw = A.ring("aw", 4, [128, 8, 512], F32)
        mrow = A.ring("mrow", 2, [2, 512], F32)
        P.dma(ccs[:, :, :], cc_in[:, :, :])
        P.act(scc[:, :, :], ccs[:, :, :], AF.Silu)
        P.dma(adab[0:1, 0:6 * D], V(ada_b.t[0:1, :], ()))
        P.dma(adab[0:1, 6 * D:12 * D], V(ada_b.t[1:2, :], ()))
        items = [(l, g) for l in range(2) for g in range(12)]

        def ld(n):
            l, g = items[n]
            src = ada_w.t[l, :, g * 512:(g + 1) * 512].rearrange("(c p) j -> p c j", p=128)
            P.dma(aw[n % 4][:, :, :], V(src, ()))

        for n in range(3):
            ld(n)
        for n, (l, g) in enumerate(items):
            if n + 3 < len(items):
                ld(n + 3)
            awt = aw[n % 4]
            mr = mrow[n % 2]
            po = V(PS[n % 2].t[0:2, 0:512], ((PS[n % 2].name, 0),))
            for kc in range(8):
                P.mm(po, scc[:, kc, :], awt[:, kc, :], start=(kc == 0), stop=False)
            P.mm(po, V(consts.t[0:1, C_ONE:C_ONE + 2], consts[:, :].keys),
                 adab[0:1, l * 6 * D + g * 512:l * 6 * D + (g + 1) * 512], start=False, stop=True)
            P.copy(mr[:, :], po, eng="act")
            P.dma(V(modD.t[l, :, g * 512:(g + 1) * 512], ((modD.name, l, g),)), mr[:, :], eng="act")

    def mod_bc(dst, l, s, j):
        si = 0 if s == "x" else 1
        P.dma(dst, V(modD.t[l, si, j * D:(j + 1) * D].partition_broadcast(128), ((modD.name, l, 2 * j), (modD.name, l, 2 * j + 1))))

    def run_staged(stages, NBLK):
        maxd = max(dl for dl, _ in stages)
        for it in range(NBLK + maxd):
            for dl, fn in stages:
                n = it - dl
                if 0 <= n < NBLK:
                    fn(n)

    def phase_norm(l, which, SRC, streams):
        A.reset()
        nw_in = norm1_w if which == 1 else norm2_w
        nwb = A.alloc("nwb", [128, D], F32)
        bc_load(nwb[:, :], nw_in.t[l, :])
        Abc = {}
        SHbc = {}
        for s in streams:
            Abc[s] = A.alloc("Abc" + s, [128, D], F32)
            SHbc[s] = A.alloc("SHbc" + s, [128, D], F32)
            mod_bc(Abc[s][:, :], l, s, 1 if which == 1 else 4)
            mod_bc(SHbc[s][:, :], l, s, 0 if which == 1 else 3)
            P.stt(Abc[s][:, :], Abc[s][:, :], 1.0, nwb[:, :], ALU.add, ALU.mult)
        xr = A.ring("xr", 4, [128, D], F32)
        junk = A.alloc("junk", [128, D], BF16)
        st = A.ring("st", 3, [128, 4], F32)
        hf = A.ring("hf", 2, [128, D], F32)
        hb = A.ring("hb", 2, [128, D], BF16)
        hT = A.ring("hT", 3, [128, 8, 128], BF16)
        blocks = [(s, j) for s in streams for j in range(NB[s])]

        def load(n):
            s, j = blocks[n]
            P.dma(xr[n % 4][:, :], V(SRC[s].t[j * 128:(j + 1) * 128, :], ((SRC[s].name, j),)))

        def N0(n):
            x_ = xr[n % 4]
            r = n % 3
            P.act(junk[:, :], x_[:, :], AF.Square, accum_out=st[r][:, 0:1])
            P.ts(st[r][:, 1:2], st[r][:, 0:1], 1.0 / D, ALU.mult, EPS, ALU.add)
            P.act(st[r][:, 2:3], st[r][:, 1:2], AF.Sqrt)
            P.recip(st[r][:, 3:4], st[r][:, 2:3])

        def N1(n):
            s, j = blocks[n]
            x_ = xr[n % 4]
            r = n % 2
            P.stt(hf[r][:, :], x_[:, :], st[n % 3][:, 3:4], Abc[s][:, :], ALU.mult, ALU.mult)
            P.tt(hb[r][:, :], hf[r][:, :], SHbc[s][:, :], ALU.add, eng="pool")

        def N2(n):
            s, j = blocks[n]
            r = n % 2
            pt = psb(2 + (n % 2), 0)
            for c in range(8):
                P.tr(V(pt.ap[:, c, :], pt.keys), hb[r][:, c * 128:(c + 1) * 128], identb[:, :])
            P.copy(hT[n % 3][:, :, :], pt, eng="act")
            P.dma(V(HT[s].t[j], ((HT[s].name, j),)),
                  V(hT[n % 3].t[:, :, :].rearrange("p c t -> p (c t)"), hT[n % 3][:, :, :].keys), eng="act")

        run_staged([(3, N2), (2, N1), (1, N0), (0, load)], len(blocks))

    def phase_gla(l, d):
        A.reset()
        di = 0 if d == "f" else 1
        Wq = A.alloc("Wq", [128, 8, D], BF16)
        Wf = A.alloc("Wf", [128, 8, D], BF16)
        Wi = A.alloc("Wi", [128, 8, D], BF16)
        load_w(Wq, WB_in[l], 0, D)
        load_w(Wf, WB_in[l], (1 + di) * D, D)
        load_w(Wi, WB_in[l], 3 * D, D)
        oml = A.alloc("oml", [128, D], F32)
        if l == 0:
            P.memset(oml[:, :], 1.0)
        else:
            t0 = A.alloc("lb0", [128, D], F32)
            bc_load(t0[:, :], lbraw.t[0, di * D:(di + 1) * D])
            bc_load(oml[:, :], lbraw.t[1, di * D:(di + 1) * D])
            P.tt(oml[:, :], t0[:, :], oml[:, :], ALU.subtract)
            P.act(oml[:, :], oml[:, :], AF.Sigmoid)
        Vst = A.alloc("Vst", [128, 8, 128], F32)
        P.memset(Vst[:, :, :], 0.0)
        ones8 = A.alloc("ones8", [128, 8], F32)
        P.memset(ones8[:, :], 1.0)
        hT = A.ring("hT", 3, [128, 8, 128], BF16)
        qs = A.alloc("qs", [128, D], F32)
        sgq = A.alloc("sgq", [128, D], F32)
        sg = A.alloc("sg", [128, D], F32)
        kk = A.alloc("kk", [128, D], F32)
        gg = A.alloc("gg", [128, D], F32)
        ib = A.ring("ib", 2, [128, D], BF16)
        E1 = A.alloc("E1", [128, D], F32)
        E1n = A.alloc("E1n", [128, D], F32)
        qtil = A.alloc("qtil", [128, D], BF16)
        ktil = A.ring("ktil", 2, [128, D], BF16)
        ev = A.ring("ev", 3, [128, 8, 4], F32)
        qT = A.ring("qT", 2, [128, 8, 128], BF16)
        kT = A.ring("kT", 2, [128, 8, 128], BF16)
        scm = A.ring("scm", 2, [128, 8, 128], BF16)
        cvec = A.ring("cvec", 2, [128, 8], F32)
        Ub = A.ring("Ub", 2, [128, 8, 128], BF16)
        osb = A.ring("osb", 2, [128, D], F32)
        mda = cst(C_MDA[di])
        mask = V(consts.t[:, C_MASK[di]:C_MASK[di] + 128].bitcast(mybir.dt.uint32).unsqueeze(1).to_broadcast([128, 8, 128]),
                 consts[:, :].keys)
        for b_ in scm:
            P.memset(b_[:, :, :], 0.0)
        csel = cst(C_SEL[di], 4)
        blocks = [("c", j) for j in range(NBC)] + [("x", j) for j in range(NBX)]
        if d == "b":
            blocks = [("c", j) for j in reversed(range(NBC))] + [("x", j) for j in reversed(range(NBX))]
        corder = (0, 1) if d == "f" else (1, 0)
        NBLK = len(blocks)
        state = {"prev_ebr": ones8[:, :]}

        def load(n):
            s, j = blocks[n]
            P.dma(V(hT[n % 3].t[:, :, :].rearrange("p c t -> p (c t)"), hT[n % 3][:, :, :].keys),
                  V(HT[s].t[j], ((HT[s].name, j),)))

        def proj(n, W, pi):
            h = hT[n % 3]
            for hh in range(2):
                for kc in range(8):
                    P.mm(psf(pi, hh), h[:, kc, :], W[:, kc, hh * 512:(hh + 1) * 512], start=(kc == 0), stop=(kc == 7))

        def A1(n):
            proj(n, Wf, 1)
            P.act(sg[:, :], psf(1), AF.Sigmoid, scale=-1.0)
            P.tt(kk[:, :], sg[:, :], oml[:, :], ALU.mult)

        def A1b(n):
            P.act(gg[:, :], kk[:, :], AF.Ln, scale=-1.0, bias=1.0)

        def A2(n):
            proj(n, Wq, 0)
            P.act(sgq[:, :], psf(0), AF.Sigmoid)
            P.tt(qs[:, :], psf(0), sgq[:, :], ALU.mult)

        def A3(n):
            proj(n, Wi, 1)
            P.copy(ib[n % 2][:, :], psf(1), eng="dve")

        def A4(n):
            r = n % 2
            for hh in range(2):
                P.mm(psf(0, hh), mda, gg[:, hh * 512:(hh + 1) * 512])
            pv = V(PS[1].t[:, 0:32].rearrange("p (h c) -> p h c", c=4), ((PS[1].name, 0),))
            for hd in range(8):
                P.mm(V(pv.ap[:, hd, :], pv.keys), gg[:, hd * 128:(hd + 1) * 128], csel)
            e = ev[n % 3]
            P.act(e[:, :, :], pv, AF.Exp)
            P.act(E1[:, :], psf(0), AF.Exp)
            P.act(E1n[:, :], psf(0), AF.Exp, scale=-1.0)
            P.tt(qtil[:, :], qs[:, :], E1[:, :], ALU.mult)
            P.tt(ktil[r][:, :], kk[:, :], E1n[:, :], ALU.mult, eng="pool")

        def A5(n):
            r = n % 2
            ptq = psb(1, 1)
            for c in range(8):
                P.tr(V(ptq.ap[:, c, :], ptq.keys), qtil[:, c * 128:(c + 1) * 128], identb[:, :])
            P.copy(qT[r][:, :, :], ptq, eng="act")
            ptk = psb(0, 0)
            for c in range(8):
                P.tr(V(ptk.ap[:, c, :], ptk.keys), ktil[r][:, c * 128:(c + 1) * 128], identb[:, :])
            P.copy(kT[r][:, :, :], ptk, eng="dve")

        def B1(n):
            r = n % 2
            sc = ps3(2)
            for hd in range(8):
                P.mm(V(sc.ap[:, hd, :], ((PS[2].name, hd // 4),)), kT[r][:, hd, :], qT[r][:, hd, :])
            P.cpred(scm[r][:, :, :], mask, sc)

        def Bpre(n, ci):
            e = ev[n % 3]
            cv = cvec[ci]
            P.tt(cv[:, :], state["prev_ebr"], e[:, :, ci], ALU.mult)
            P.tt(Vst[:, :, :], Vst[:, :, :],
                 V(cv.t[:, :].unsqueeze(2).to_broadcast([128, 8, 128]), cv[:, :].keys), ALU.mult)
            P.copy(Ub[ci][:, :, :], Vst[:, :, :], eng="act")

        def Bmm(n, ci):
            r = n % 2
            e = ev[n % 3]
            lo, hi = 64 * ci, 64 * ci + 64
            kv = ps3(2)
            for hd in range(8):
                P.mm(V(kv.ap[:, hd, :], ((PS[2].name, hd // 4),)),
                     ktil[r][lo:hi, hd * 128:(hd + 1) * 128], ib[r][lo:hi, hd * 128:(hd + 1) * 128])
            for hd in range(8):
                oo = V(PS[3].t[lo:hi, hd * 128:(hd + 1) * 128], ((PS[3].name, hd // 4),))
                P.mm(oo, scm[r][:, hd, lo:hi], ib[r][:, hd * 128:(hd + 1) * 128], start=True, stop=False)
                P.mm(oo, qT[r][:, hd, lo:hi], Ub[ci][:, hd, :], start=False, stop=True)
            P.tt(Vst[:, :, :], Vst[:, :, :], kv, ALU.add)
            state["prev_ebr"] = e[:, :, 2 + ci]

        def B3(n):
            r = n % 2
            s, j = blocks[n]
            P.copy(osb[r][:, :], psf(3), eng="act")
            P.dma(V(OD[(s, d)].t[j], ((OD[(s, d)].name, j),)), osb[r][:, :], eng="act")

        load(0)
        load(1)
        for st_ in (A1, A2, A1b, A3, A4):
            st_(0)
        for n in range(NBLK):
            nx = n + 1 if n + 1 < NBLK else None
            if n + 2 < NBLK:
                load(n + 2)
            if nx is not None:
                A1(nx)
                A2(nx)
            A5(n)
            B1(n)
            Bpre(n, corder[0])
            if nx is not None:
                A1b(nx)
            Bmm(n, corder[0])
            Bpre(n, corder[1])
            if nx is not None:
                A3(nx)
            Bmm(n, corder[1])
            if nx is not None:
                A4(nx)
            B3(n)

    def phase_read(l, streams):
        A.reset()
        Wog = A.alloc("Wog", [128, 8, D], BF16)
        Wgb = A.alloc("Wgb", [128, 8, D], BF16)
        Wb = A.alloc("Wb", [128, 8, D], BF16)
        load_w(Wog, WB_in[l], 6 * D, D)
        load_w(Wgb, WB_in[l], 8 * D, D)
        load_w(Wb, WB_b[l], 0, D)
        hncol = A.alloc("hncol", [128, 1], F32)
        P.dma(hncol[:, :], V(hnw.t[l, :].rearrange("(p o) -> p o", o=1), ()))
        for kc in range(8):
            P.act(Wb[:, kc, :], Wb[:, kc, :], AF.Copy, scale=hncol[:, 0:1])
        hT = A.ring("hT", 3, [128, 8, 128], BF16)
        of = A.ring("of", 3, [128, D], F32)
        ob = A.ring("ob", 3, [128, D], F32)
        o = A.ring("o", 4, [128, D], F32)
        o2 = A.alloc("o2", [128, D], F32)
        st = A.ring("st", 4, [128, 8, 4], F32)
        on = A.alloc("on", [128, D], F32)
        sog = A.ring("sog", 5, [128, D], F32)
        sgo = A.alloc("sgo", [128, D], F32)
        yb = A.ring("yb", 3, [128, D], BF16)
        ybT = A.ring("ybT", 3, [128, 8, 128], BF16)
        sgb = A.ring("sgb", 10, [128, D], F32)
        mb = A.ring("mb", 2, [128, D], F32)
        blocks = [(s, j) for s in streams for j in range(NB[s])]
        NBLK = len(blocks)

        def Lh(n):
            s, j = blocks[n]
            r = n % 3
            P.dma(V(hT[r].t[:, :, :].rearrange("p c t -> p (c t)"), hT[r][:, :, :].keys), V(HT[s].t[j], ((HT[s].name, j),)))
            P.dma(of[r][:, :], V(OD[(s, "f")].t[j], ((OD[(s, "f")].name, j),)))
            P.dma(ob[r][:, :], V(OD[(s, "b")].t[j], ((OD[(s, "b")].name, j),)))

        def SA(n):
            h = hT[n % 3]
            for W, pi in ((Wog, 0), (Wgb, 1)):
                for hh in range(2):
                    for kc in range(8):
                        P.mm(psf(pi, hh), h[:, kc, :], W[:, kc, hh * 512:(hh + 1) * 512], start=(kc == 0), stop=(kc == 7))
            P.act(sgo[:, :], psf(0), AF.Sigmoid)
            P.act(sgb[n % 10][:, :], psf(1), AF.Sigmoid)
            P.tt(sog[n % 5][:, :], psf(0), sgo[:, :], ALU.mult)

        def SB(n):
            o_ = o[n % 4]
            s_ = st[n % 4]
            P.tt(o_[:, :], of[n % 3][:, :], ob[n % 3][:, :], ALU.add)
            P.tt(o2[:, :], o_[:, :], o_[:, :], ALU.mult)
            P.add("dve", lambda e: e.tensor_reduce(s_.t[:, :, 0], o2.t[:, :].rearrange("p (h v) -> p h v", v=128), AX.X, ALU.add),
                  [o2[:, :]], [s_[:, :, :]])
            P.ts(s_[:, :, 1], s_[:, :, 0], 1.0 / 128, ALU.mult, EPS, ALU.add)

        def SB2(n):
            s_ = st[n % 4]
            P.act(s_[:, :, 2], s_[:, :, 1], AF.Sqrt)

        def SB3(n):
            o_ = o[n % 4]
            s_ = st[n % 4]
            P.recip(s_[:, :, 3], s_[:, :, 2])
            o3 = V(o_.t[:, :].rearrange("p (h v) -> p h v", v=128), o_[:, :].keys)
            on3 = V(on.t[:, :].rearrange("p (h v) -> p h v", v=128), on[:, :].keys)
            P.tt(on3, o3, V(s_.t[:, :, 3:4].to_broadcast([128, 8, 128]), s_[:, :, :].keys), ALU.mult)
            P.tt(yb[n % 3][:, :], on[:, :], sog[n % 5][:, :], ALU.mult)

        def SC(n):
            pt = psb(2, 0)
            for c in range(8):
                P.tr(V(pt.ap[:, c, :], pt.keys), yb[n % 3][:, c * 128:(c + 1) * 128], identb[:, :])

        def SD(n):
            P.copy(ybT[n % 3][:, :, :], psb(2, 0), eng="act")

        def SE(n):
            for hh in range(2):
                for kc in range(8):
                    P.mm(psf(3, hh), ybT[n % 3][:, kc, :], Wb[:, kc, hh * 512:(hh + 1) * 512], start=(kc == 0), stop=(kc == 7))

        def SF(n):
            s, j = blocks[n]
            r = n % 2
            P.tt(mb[r][:, :], psf(3), sgb[n % 10][:, :], ALU.mult)
            P.dma(V(MB[s].t[j], ((MB[s].name, j),)), mb[r][:, :])

        run_staged([(8, SF), (6, SD), (0, Lh), (3, SB2), (2, SB), (4, SB3), (7, SE), (5, SC), (1, SA)], NBLK)

    def phase_sgu(l, streams):
        A.reset()
        Wu = A.alloc("Wu", [128, 8, D], BF16)
        Wv = A.alloc("Wv", [128, 8, D], BF16)
        Wga = A.alloc("Wga", [128, 8, D], BF16)
        Wa = A.alloc("Wa", [128, 8, D], BF16)
        Wo = A.alloc("Wo", [128, 8, D], BF16)
        load_w(Wu, WB_in[l], 4 * D, D)
        load_w(Wv, WB_in[l], 5 * D, D)
        load_w(Wga, WB_in[l], 7 * D, D)
        load_w(Wa, WB_a[l], 0, D)
        load_w(Wo, WB_o[l], 0, D)
        swf = A.alloc("swf", [128, 8, 128], F32)
        swT = A.alloc("swT", [128, 8, 128], BF16)
        P.dma(swf[:, :, :], V(sgu_wT.t[l], ()))
        P.copy(swT[:, :, :], swf[:, :, :], eng="pool")
        bsf = A.alloc("bsf", [1, D], F32)
        bsh = A.alloc("bsh", [1, D], BF16)
        bsr = A.alloc("bsr", [1, D], F32)
        bsl = A.alloc("bsl", [1, D], BF16)
        oneb = A.alloc("oneb", [1, 128], BF16)
        P.dma(bsf[:, :], V(sgu_b.t[l:l + 1, :], ()))
        P.copy(bsh[:, :], bsf[:, :], eng="dve")
        P.tt(bsr[:, :], bsf[:, :], bsh[:, :], ALU.subtract)
        P.copy(bsl[:, :], bsr[:, :], eng="dve")
        P.memset(oneb[:, :], 1.0)
        lnw = A.alloc("lnw", [128, D], F32)
        lnb = A.alloc("lnb", [128, D], F32)
        bc_load(lnw[:, :], sgu_ln_w.t[l, :])
        bc_load(lnb[:, :], sgu_ln_b.t[l, :])
        G1 = {}
        for s in streams:
            G1[s] = A.alloc("G1" + s, [128, D], F32)
            mod_bc(G1[s][:, :], l, s, 2)
        hT = A.ring("hT", 4, [128, 8, 128], BF16)
        mbl = A.ring("mbl", 2, [128, D], F32)
        xr = A.ring("xr", 2, [128, D], F32)
        gv = A.ring("gv", 2, [128, D], F32)
        junk = A.alloc("junk", [128, D], BF16)
        st = A.ring("st", 3, [128, 8], F32)
        vn = A.ring("vn", 2, [128, D], BF16)
        t1 = A.alloc("t1", [128, D], F32)
        t2 = A.alloc("t2", [128, D], F32)
        t3 = A.alloc("t3", [128, D], F32)
        guT = A.ring("guT", 3, [128, 8, 128], BF16)
        yaT = A.ring("yaT", 2, [128, 8, 128], BF16)
        sga = A.ring("sga", 2, [128, D], F32)
        mg = A.ring("mg", 2, [128, D], BF16)
        mgT = A.ring("mgT", 2, [128, 8, 128], BF16)
        xo = A.ring("xo", 2, [128, D], F32)
        blocks = [(s, j) for s in streams for j in range(NB[s])]
        NBLK = len(blocks)

        def Lh(n):
            s, j = blocks[n]
            r = n % 4
            P.dma(V(hT[r].t[:, :, :].rearrange("p c t -> p (c t)"), hT[r][:, :, :].keys), V(HT[s].t[j], ((HT[s].name, j),)))

        def Lm(n):
            s, j = blocks[n]
            P.dma(mbl[n % 2][:, :], V(MB[s].t[j], ((MB[s].name, j),)))

        def Lx(n):
            s, j = blocks[n]
            P.dma(xr[n % 2][:, :], V(XA[s].t[j * 128:(j + 1) * 128, :], ((XA[s].name, j),)))

        def S0a(n):
            h = hT[n % 4]
            r = n % 2
            for hh in range(2):
                for kc in range(8):
                    P.mm(psf(0, hh), h[:, kc, :], Wv[:, kc, hh * 512:(hh + 1) * 512], start=(kc == 0), stop=(kc == 7))
            s_ = st[n % 3]
            P.act(gv[r][:, :], psf(0), AF.Gelu_apprx_tanh, accum_out=s_[:, 0:1])
            P.act(junk[:, :], gv[r][:, :], AF.Square, accum_out=s_[:, 1:2])

        def S0b(n):
            h = hT[n % 4]
            pu = ps3(1)
            for c in range(8):
                for kc in range(8):
                    P.mm(V(pu.ap[:, c, :], ((PS[1].name, c // 4),)), Wu[:, kc, c * 128:(c + 1) * 128], h[:, kc, :],
                         start=(kc == 0), stop=(kc == 7))
            P.act(guT[n % 3][:, :, :], pu, AF.Gelu_apprx_tanh)

        def S1(n):
            r = n % 2
            s_ = st[n % 3]
            P.ts(s_[:, 2:3], s_[:, 0:1], 1.0 / D, ALU.mult)
            P.ts(s_[:, 3:4], s_[:, 1:2], 1.0 / D, ALU.mult, LN_EPS, ALU.add)
            P.tt(s_[:, 4:5], s_[:, 2:3], s_[:, 2:3], ALU.mult)
            P.tt(s_[:, 5:6], s_[:, 3:4], s_[:, 4:5], ALU.subtract)
            P.act(s_[:, 6:7], s_[:, 5:6], AF.Sqrt)
            P.recip(s_[:, 7:8], s_[:, 6:7])
            P.ts(t1[:, :], gv[r][:, :], s_[:, 2:3], ALU.subtract, s_[:, 7:8], ALU.mult)
            P.tt(t1[:, :], t1[:, :], lnw[:, :], ALU.mult, eng="pool")
            P.tt(vn[r][:, :], t1[:, :], lnb[:, :], ALU.add, eng="pool")

        def S2(n):
            r = n % 2
            h = hT[n % 4]
            pm = ps3(2)
            for g in range(8):
                og_ = V(pm.ap[:, g, :], ((PS[2].name, g // 4),))
                P.mm(og_, vn[r][:, g * 128:(g + 1) * 128], swT[:, g, :], start=True, stop=False)
                P.mm(og_, oneb[0:1, :], bsh[0:1, g * 128:(g + 1) * 128], start=False, stop=False)
                P.mm(og_, oneb[0:1, :], bsl[0:1, g * 128:(g + 1) * 128], start=False, stop=True)
            P.tt(yaT[r][:, :, :], pm, guT[n % 3][:, :, :], ALU.mult)
            for hh in range(2):
                for kc in range(8):
                    P.mm(psf(0, hh), h[:, kc, :], Wga[:, kc, hh * 512:(hh + 1) * 512], start=(kc == 0), stop=(kc == 7))
            P.act(sga[r][:, :], psf(0), AF.Sigmoid)

        def S3(n):
            r = n % 2
            for hh in range(2):
                for kc in range(8):
                    P.mm(psf(3, hh), yaT[r][:, kc, :], Wa[:, kc, hh * 512:(hh + 1) * 512], start=(kc == 0), stop=(kc == 7))
            P.tt(t2[:, :], psf(3), sga[r][:, :], ALU.mult)
            P.tt(mg[r][:, :], t2[:, :], mbl[r][:, :], ALU.add, eng="pool")

        def S4(n):
            r = n % 2
            pt = psb(2, 0)
            for c in range(8):
                P.tr(V(pt.ap[:, c, :], pt.keys), mg[r][:, c * 128:(c + 1) * 128], identb[:, :])
            P.copy(mgT[r][:, :, :], pt, eng="act")

        def S5(n):
            s, j = blocks[n]
            r = n % 2
            for hh in range(2):
                for kc in range(8):
                    P.mm(psf(3, hh), mgT[r][:, kc, :], Wo[:, kc, hh * 512:(hh + 1) * 512], start=(kc == 0), stop=(kc == 7))
            P.tt(t3[:, :], psf(3), G1[s][:, :], ALU.mult)
            P.tt(xo[r][:, :], t3[:, :], xr[r][:, :], ALU.add, eng="pool")
            P.dma(V(XMID[s].t[j * 128:(j + 1) * 128, :], ((XMID[s].name, j),)), xo[r][:, :])

        run_staged([(0, Lh), (3, Lm), (5, Lx), (2, S1), (4, S3), (1, S0a), (5, S4), (1, S0b), (6, S5), (3, S2)], NBLK)

    def phase_ffn1(l, streams):
        A.reset()
        Wua = A.alloc("Wua", [128, 8, DFF], BF16)
        load_w(Wua, WB_up[l], 0, DFF)
        cwf = A.alloc("cwf", [128, NCC, 9], F32)
        cbf = A.alloc("cbf", [128, NCC], F32)
        P.dma(cwf[:, :, :], V(convw.t[l], ()))
        P.dma(cbf[:, :], V(convb.t[l], ()))
        dg = A.alloc("dg", [128, NCC, 9, 128], BF16)
        idb = V(consts.t[:, C_ID:C_ID + 128].unsqueeze(1).to_broadcast([128, NCC, 128]), consts[:, :].keys)
        for tp in range(9):
            P.tt(dg[:, :, tp, :], idb, V(cwf.t[:, :, tp:tp + 1].to_broadcast([128, NCC, 128]), cwf[:, :, :].keys),
                 ALU.mult, eng=("dve" if tp % 2 == 0 else "pool"))
        h2 = A.ring("h2", 2, [128, 8, 640], BF16)
        apx = A.ring("apx", 3, [128, 10, 66], BF16)
        apc = A.ring("apc", 3, [128, 258], BF16)
        for b_ in apx:
            P.memset(b_[:, :, :], 0.0)
        for b_ in apc:
            P.memset(b_[:, :], 0.0)
        gat = A.ring("gat", 2, [128, NCC, 512], BF16)
        tiles = []
        for s in streams:
            tiles += [(s, t) for t in range(8)] if s == "x" else [("c", 0)]

        def loads(n):
            s, t = tiles[n]
            hh = h2[n % 2]
            if s == "x":
                for b in range(-1, 5):
                    j = 4 * t + b
                    if j < 0 or j >= NBX:
                        continue
                    src = HT[s].t[j].rearrange("p (c t) -> p c t", t=128)
                    if b == -1:
                        P.dma(hh[:, :, 0:64], V(src[:, :, 64:128], ((HT[s].name, j),)))
                    elif b == 4:
                        P.dma(hh[:, :, 576:640], V(src[:, :, 0:64], ((HT[s].name, j),)))
                    else:
                        P.dma(hh[:, :, 64 + 128 * b:64 + 128 * (b + 1)], V(src, ((HT[s].name, j),)))
            else:
                for j in range(NBC):
                    src = HT[s].t[j].rearrange("p (c t) -> p c t", t=128)
                    P.dma(hh[:, :, 128 * j:128 * (j + 1)], V(src, ((HT[s].name, j),)))

        items = [(n, cc) for n in range(len(tiles)) for cc in range(NCC)]

        def stA(i):
            n, cc = items[i]
            s, t = tiles[n]
            if cc == 0:
                if n == 0:
                    loads(0)
                if n + 1 < len(tiles):
                    loads(n + 1)
            hh = h2[n % 2]
            wsl = lambda kc: Wua[:, kc, cc * 128:(cc + 1) * 128]
            if s == "x":
                ap_ = apx[i % 3]
                pa = psf(i % 2, 0)
                ph = psf(i % 2, 1)
                for kc in range(8):
                    P.mm(pa, wsl(kc), hh[:, kc, 64:576], start=(kc == 0), stop=(kc == 7))
                P.copy(ap_[:, 1:9, 1:65], V(pa.ap.rearrange("p (r w) -> p r w", w=64), pa.keys), eng="act")
                if t > 0:
                    for kc in range(8):
                        P.mm(V(ph.ap[:, 0:64], ph.keys), wsl(kc), hh[:, kc, 0:64], start=(kc == 0), stop=(kc == 7))
                    P.copy(ap_[:, 0, 1:65], V(ph.ap[:, 0:64], ph.keys), eng="dve")
                else:
                    P.memset(ap_[:, 0, 1:65], 0.0)
                if t < 7:
                    for kc in range(8):
                        P.mm(V(ph.ap[:, 64:128], ph.keys), wsl(kc), hh[:, kc, 576:640], start=(kc == 0), stop=(kc == 7))
                    P.copy(ap_[:, 9, 1:65], V(ph.ap[:, 64:128], ph.keys), eng="dve")
                else:
                    P.memset(ap_[:, 9, 1:65], 0.0)
            else:
                ap_ = apc[i % 3]
                pa = V(PS[i % 2].t[:, 0:256], ((PS[i % 2].name, 0),))
                for kc in range(8):
                    P.mm(pa, wsl(kc), hh[:, kc, 0:256], start=(kc == 0), stop=(kc == 7))
                P.copy(ap_[:, 1:257], pa, eng="act")

        def stB(i):
            n, cc = items[i]
            s, t = tiles[n]
            g_ = gat[n % 2]
            if s == "x":
                ap_ = apx[i % 3]
                pc = psf(2 + i % 2, 0)
                pc3 = V(pc.ap.rearrange("p (r w) -> p r w", w=64), pc.keys)
                for tp in range(9):
                    dy, dx = tp // 3, tp % 3
                    P.mm(pc3, dg[:, cc, tp, :], ap_[:, dy:dy + 8, dx:dx + 64], start=(tp == 0), stop=(tp == 8))
                P.act(g_[:, cc, :], pc, AF.Gelu_apprx_tanh, bias=cbf[:, cc:cc + 1])
            else:
                ap_ = apc[i % 3]
                pc = V(PS[2 + i % 2].t[:, 0:256], ((PS[2 + i % 2].name, 0),))
                for dx in range(3):
                    P.mm(pc, dg[:, cc, 3 + dx, :], ap_[:, dx:dx + 256], start=(dx == 0), stop=(dx == 2))
                P.act(g_[:, cc, 0:256], pc, AF.Gelu_apprx_tanh, bias=cbf[:, cc:cc + 1])
            if cc == NCC - 1:
                if s == "x":
                    P.dma(V(GA[s].t[t], ((GA[s].name, t),)), g_[:, :, :], eng="act")
                else:
                    P.dma(V(GA[s].t[0], ((GA[s].name, 0),)), g_[:, :, 0:256], eng="act")

        run_staged([(2, stB), (0, stA)], len(items))

    def phase_ffn2(l, streams, last):
        A.reset()
        Wuv = A.alloc("Wuv", [128, 8, DFF], BF16)
        Wd = A.alloc("Wd", [128, NCC, D], BF16)
        load_w(Wuv, WB_up[l], DFF, DFF)
        P.dma(Wd[:, :, :], V(WB_dn[l].t[:, :].rearrange("(c p) j -> p c j", p=128), wkeys(WB_dn[l], 0, D, DFF)))
        G2 = {}
        for s in streams:
            G2[s] = A.alloc("G2" + s, [128, D], F32)
            mod_bc(G2[s][:, :], l, s, 5)
        if last:
            fnb = A.alloc("fnb", [128, D], F32)
            bc_load(fnb[:, :], fnw.t[0, :])
        h2 = A.ring("h2", 2, [128, 8, 512], BF16)
        gal = A.ring("gal", 2, [128, NCC, 512], BF16)
        gvv = A.alloc("gvv", [128, NCC, 512], BF16)
        xr = A.ring("xr", 2, [128, D], F32)
        tmp = A.alloc("tmp", [128, D], F32)
        xo = A.ring("xo", 2, [128, D], F32)
        junk = A.alloc("junk", [128, D], BF16)
        st = A.ring("st", 2, [128, 4], F32)
        tiles = []
        for s in streams:
            tiles += [(s, t) for t in range(8)] if s == "x" else [("c", 0)]

        def loads(n):
            s, t = tiles[n]
            nb = 4 if s == "x" else 2
            ntok = nb * 128
            for b in range(nb):
                j = 4 * t + b
                src = HT[s].t[j].rearrange("p (c t) -> p c t", t=128)
                P.dma(h2[n % 2][:, :, 128 * b:128 * (b + 1)], V(src, ((HT[s].name, j),)))
            P.dma(gal[n % 2][:, :, 0:ntok], V(GA[s].t[t], ((GA[s].name, t),)))

        nblk = 0
        for n, (s, t) in enumerate(tiles):
            if n == 0:
                loads(0)
            if n + 1 < len(tiles):
                loads(n + 1)
            nb = 4 if s == "x" else 2
            ntok = nb * 128
            hh = h2[n % 2]
            for cc in range(NCC):
                pv = V(PS[cc % 2].t[:, 0:ntok], ((PS[cc % 2].name, 0),))
                for kc in range(8):
                    P.mm(pv, Wuv[:, kc, cc * 128:(cc + 1) * 128], hh[:, kc, 0:ntok], start=(kc == 0), stop=(kc == 7))
                P.tt(gvv[:, cc, 0:ntok], pv, gal[n % 2][:, cc, 0:ntok], ALU.mult)
            for b in range(nb):
                j = 4 * t + b
                r = nblk % 2
                nblk += 1
                P.dma(xr[r][:, :], V(XMID[s].t[j * 128:(j + 1) * 128, :], ((XMID[s].name, j),)))
                py = 2 + (nblk % 2)
                for hf in range(2):
                    for cc in range(NCC):
                        P.mm(psf(py, hf), gvv[:, cc, b * 128:(b + 1) * 128], Wd[:, cc, hf * 512:(hf + 1) * 512],
                             start=(cc == 0), stop=(cc == NCC - 1))
                P.tt(tmp[:, :], psf(py), G2[s][:, :], ALU.mult)
                if not last:
                    P.tt(xo[r][:, :], tmp[:, :], xr[r][:, :], ALU.add, eng="pool")
                    P.dma(V(XN[s].t[j * 128:(j + 1) * 128, :], ((XN[s].name, j),)), xo[r][:, :])
                else:
                    P.tt(tmp[:, :], tmp[:, :], xr[r][:, :], ALU.add, eng="pool")
                    s_ = st[r]
                    P.act(junk[:, :], tmp[:, :], AF.Square, accum_out=s_[:, 0:1])
                    P.ts(s_[:, 1:2], s_[:, 0:1], 1.0 / D, ALU.mult, EPS, ALU.add)
                    P.act(s_[:, 2:3], s_[:, 1:2], AF.Sqrt)
                    P.recip(s_[:, 3:4], s_[:, 2:3])
                    P.stt(xo[r][:, :], tmp[:, :], s_[:, 3:4], fnb[:, :], ALU.mult, ALU.mult)
                    P.dma(V(out.t[j * 128:(j + 1) * 128, :], ((out.name, j),)), xo[r][:, :])

    def done(tag):
        return stop_after == tag

    def run():
        phase_mod()
        P.barrier()
        if done("mod"):
            return
        for l in range(2):
            last = l == 1
            global_streams = "xc"
            if l == 1:
                XA["x"] = XN["x"]
                XA["c"] = XN["c"]
            phase_norm(l, 1, XA, "cx")
            P.barrier()
            if done("n1_%d" % l):
                return
            phase_gla(l, "b")
            P.barrier()
            if done("gb_%d" % l):
                return
            phase_gla(l, "f")
            P.barrier()
            if done("gf_%d" % l):
                return
            streams = "x" if last else "xc"
            phase_read(l, streams)
            P.barrier()
            if done("rd_%d" % l):
                return
            phase_sgu(l, streams)
            P.barrier()
            if done("sg_%d" % l):
                return
            phase_norm(l, 2, XMID, streams)
            P.barrier()
            phase_ffn1(l, streams)
            P.barrier()
            if done("f1_%d" % l):
                return
            phase_ffn2(l, streams, last)
            P.barrier()
            if done("f2_%d" % l):
                return

    run()
    P.barrier()
    P.emit()
    return nc


def make_in_maps(inputs):
    f = lambda a: np.ascontiguousarray(np.asarray(a, dtype=np.float32))
    x = f(inputs["x"])
    c = f(inputs["c"])
    ctx = f(inputs["ctx"])
    c_ctx = f(inputs["c_ctx"])
    shared = {}
    for k in ("ada_w", "ada_b", "norm1_w", "w_in", "sgu_ln_w", "sgu_ln_b", "hgrn_lower_bounds", "hgrn_norm_w",
              "w_branch_a", "w_branch_b", "w_out", "norm2_w", "ffn_w_up", "ffn_w_down"):
        shared[k] = f(inputs[k])
    shared["sgu_wT"] = f(np.transpose(f(inputs["sgu_w"]), (0, 3, 1, 2)))
    shared["sgu_b"] = f(f(inputs["sgu_b"]).reshape(2, D))
    cw = f(inputs["ffn_conv_w"]).reshape(2, 9, NCC, 128)
    shared["convw"] = f(np.transpose(cw, (0, 3, 2, 1)))
    shared["convb"] = f(np.transpose(f(inputs["ffn_conv_b"]).reshape(2, NCC, 128), (0, 2, 1)))
    shared["final_norm_w"] = f(f(inputs["final_norm_w"]).reshape(1, D))
    shared["consts"] = make_consts()
    maps = []
    for core in range(8):
        b = core % 4
        m = dict(shared)
        m["x"] = f(x[b])
        m["ctx"] = f(ctx[b])
        cc = np.stack([c[b].reshape(8, 128).T, c_ctx.reshape(8, 128).T], axis=-1)
        m["cc"] = f(cc)
        maps.append(m)
    return maps


_NC_CACHE = {}


def kernel(**inputs):
    if "nc" not in _NC_CACHE:
        _NC_CACHE["nc"] = build()
    nc = _NC_CACHE["nc"]
    maps = make_in_maps(inputs)
    res = run_bass_kernel_spmd(nc, maps, core_ids=list(range(8)))
    outs = [np.asarray(res.results[b]["out"], dtype=np.float32) for b in range(4)]
    return np.stack(outs, axis=0)
```

```python
import numpy as np
from contextlib import ExitStack
import concourse.bass as bass
import concourse.mybir as mybir
from concourse.bass_utils import run_bass_kernel_spmd

F32 = mybir.dt.float32
BF16 = mybir.dt.bfloat16
AF = mybir.ActivationFunctionType
ALU = mybir.AluOpType
AX = mybir.AxisListType

ENGS = ("pe", "act", "dve", "pool", "sp")
D = 1024
SEQ = 4096
CTX = 256
NBX = SEQ // 128
NBC = CTX // 128
DFF = 2816
NCC = DFF // 128
DIN = 9216
EPS = 1e-6
LN_EPS = 1e-5


class V:
    __slots__ = ("ap", "keys")

    def __init__(self, ap, keys):
        self.ap = ap
        self.keys = keys


class Buf:
    def __init__(self, t, name):
        self.t = t
        self.name = name

    def __getitem__(self, idx):
        return V(self.t[idx], (self.name,))

    def k(self, sub, idx=slice(None)):
        return V(self.t[idx], ((self.name, sub),))

    def v(self, ap, sub=None):
        return V(ap, ((self.name, sub),) if sub is not None else (self.name,))


class Op:
    __slots__ = ("eng", "fn", "reads", "writes", "dma", "deps", "ticket", "sem", "has_dep", "idx", "dq", "bar")

    def __init__(self, eng, fn, reads, writes, dma, bar=False):
        self.eng = eng
        self.fn = fn
        self.reads = reads
        self.writes = writes
        self.dma = dma
        self.deps = ()
        self.ticket = None
        self.sem = None
        self.has_dep = False
        self.bar = bar


class Prog:
    NDMA_SEM = 12

    def __init__(self, nc):
        self.nc = nc
        self.ops = []
        self.uid = 0

    def sbuf_at(self, name, shape, dtype, offset):
        self.uid += 1
        nm = "%s_%d" % (name, self.uid)
        return Buf(self.nc.alloc_sbuf_tensor_at(nm, list(shape), dtype, offset=offset), nm)

    def psum(self, name, shape, dtype=F32):
        return Buf(self.nc.alloc_psum_tensor(name, list(shape), dtype), name)

    def dram(self, name, shape, dtype, kind="Internal"):
        return Buf(self.nc.dram_tensor(name, list(shape), dtype, kind=kind), name)

    def add(self, eng, fn, reads, writes, dma=False, bar=False):
        rk = []
        for r in reads:
            if isinstance(r, V):
                rk.extend(r.keys)
        wk = []
        for w in writes:
            if isinstance(w, V):
                wk.extend(w.keys)
        op = Op(eng, fn, tuple(rk), tuple(wk), dma, bar)
        self.ops.append(op)
        return op

    @staticmethod
    def _a(x):
        return x.ap if isinstance(x, V) else x

    def mm(self, out, lhsT, rhs, start=True, stop=True):
        a = self._a
        return self.add("pe", lambda e: e.matmul(a(out), a(lhsT), a(rhs), start=start, stop=stop),
                        [lhsT, rhs], [out])

    def tr(self, out, in_, ident):
        a = self._a
        return self.add("pe", lambda e: e.transpose(a(out), a(in_), a(ident)), [in_, ident], [out])

    def act(self, out, in_, func, bias=0.0, scale=1.0, accum_out=None):
        a = self._a
        kw = {}
        if accum_out is not None:
            kw["accum_out"] = a(accum_out)
        w = [out] + ([accum_out] if accum_out is not None else [])
        return self.add("act", lambda e: e.activation(a(out), a(in_), func, bias=a(bias), scale=a(scale), **kw),
                        [in_, bias, scale], w)

    def tt(self, out, in0, in1, op, eng="dve"):
        a = self._a
        return self.add(eng, lambda e: e.tensor_tensor(a(out), a(in0), a(in1), op), [in0, in1], [out])

    def ts(self, out, in0, s1, op0, s2=None, op1=None, eng="dve"):
        a = self._a
        if op1 is None:
            return self.add(eng, lambda e: e.tensor_scalar(a(out), a(in0), a(s1), None, op0), [in0, s1], [out])
        return self.add(eng, lambda e: e.tensor_scalar(a(out), a(in0), a(s1), a(s2), op0, op1),
                        [in0, s1, s2], [out])

    def stt(self, out, in0, scalar, in1, op0, op1, eng="dve"):
        a = self._a
        return self.add(eng, lambda e: e.scalar_tensor_tensor(a(out), a(in0), a(scalar), a(in1), op0, op1),
                        [in0, scalar, in1], [out])

    def copy(self, out, in_, eng="dve"):
        a = self._a
        if eng == "act":
            return self.add(eng, lambda e: e.activation(a(out), a(in_), AF.Copy), [in_], [out])
        return self.add(eng, lambda e: e.tensor_copy(a(out), a(in_)), [in_], [out])

    def cpred(self, out, mask, data):
        a = self._a
        return self.add("dve", lambda e: e.copy_predicated(a(out), a(mask), a(data)), [mask, data, out], [out])

    def memset(self, out, val, eng="pool"):
        a = self._a
        return self.add(eng, lambda e: e.memset(a(out), val), [], [out])

    def recip(self, out, in_):
        a = self._a
        return self.add("dve", lambda e: e.reciprocal(a(out), a(in_)), [in_], [out])

    def dma(self, out, in_, eng="sp", **kw):
        a = self._a
        return self.add(eng, lambda e: e.dma_start(out=a(out), in_=a(in_), **kw), [in_], [out], dma=True)

    def barrier(self):
        for e in ENGS:
            self.add(e, lambda eng: eng.nop(), [], [], bar=True)

    def resolve(self):
        last_w = {}
        readers = {}
        ops = self.ops
        last_on = {}
        dmas = []
        for i, op in enumerate(ops):
            op.idx = i
            deps = set()
            if op.bar:
                deps.update(last_on.values())
                deps.update(dmas)
            for k in op.reads:
                if k in last_w:
                    deps.add(last_w[k])
            for k in op.writes:
                if k in last_w:
                    deps.add(last_w[k])
                r = readers.get(k)
                if r:
                    deps.update(r.values())
            for k in op.reads:
                r = readers.setdefault(k, {})
                if op.dma:
                    r[("dma", i)] = i
                else:
                    r[op.eng] = i
            for k in op.writes:
                last_w[k] = i
                readers[k] = {}
            deps.discard(i)
            if op.eng == "pe" and not op.bar:
                deps = {d for d in deps if not (ops[d].eng == "pe" and not ops[d].dma)}
            op.deps = deps
            for d in deps:
                ops[d].has_dep = True
            if op.dma:
                if op.eng != "pool":
                    dmas.append(i)
            else:
                last_on[op.eng] = i
            if op.bar and op.eng == ENGS[-1]:
                dmas = []
        cnt = {e: 0 for e in ENGS}
        dcnt = {}
        for op in ops:
            if op.dma:
                n = dcnt.get(op.eng, 0)
                dcnt[op.eng] = n + 1
                op.dq = n
            elif op.has_dep:
                cnt[op.eng] += 1
                op.ticket = cnt[op.eng]

    def emit(self):
        nc = self.nc
        self.resolve()
        ops = self.ops
        with ExitStack() as st:
            esem = {e: st.enter_context(nc.semaphore("s_" + e)) for e in ENGS}
            dq_engs = sorted({op.eng for op in ops if op.dma})
            dsem = {e: [st.enter_context(nc.semaphore("d_%s_%d" % (e, i))) for i in range(self.NDMA_SEM)]
                    for e in dq_engs}
            K = self.NDMA_SEM
            for op in ops:
                if op.dma:
                    op.sem = dsem[op.eng][op.dq % K]
                    op.ticket = 16 * (op.dq // K + 1)
                elif op.ticket is not None:
                    op.sem = esem[op.eng]
            per_eng = {e: [op for op in ops if op.eng == e] for e in ENGS}
            block = st.enter_context(nc.Block())

            def run(engname, e):
                seen = {}
                for op in per_eng[engname]:
                    waits = {}
                    for d in op.deps:
                        dop = ops[d]
                        key = id(dop.sem)
                        if seen.get(key, 0) >= dop.ticket:
                            continue
                        if key not in waits or waits[key][1] < dop.ticket:
                            waits[key] = (dop.sem, dop.ticket)
                    if op.dma and op.ticket > 16:
                        key = id(op.sem)
                        prev = op.ticket - 16
                        if seen.get(key, 0) < prev and (key not in waits or waits[key][1] < prev):
                            waits[key] = (op.sem, prev)
                    for key, (s, tk) in waits.items():
                        e.wait_ge(s, tk)
                        seen[key] = tk
                    ins = op.fn(e)
                    if op.dma:
                        ins.then_inc(op.sem, 16)
                    elif op.ticket is not None:
                        ins.then_inc(op.sem, 1)

            @block.sync
            def _(e):
                run("sp", e)

            @block.scalar
            def _(e):
                run("act", e)

            @block.vector
            def _(e):
                run("dve", e)

            @block.gpsimd
            def _(e):
                run("pool", e)

            @block.tensor
            def _(e):
                run("pe", e)


class Arena:
    def __init__(self, P, base, limit=229312):
        self.P = P
        self.base = base
        self.cur = base
        self.limit = limit

    def reset(self):
        self.cur = self.base

    def alloc(self, name, shape, dtype):
        n = 1
        for s in shape[1:]:
            n *= s
        nbytes = n * (2 if dtype == BF16 else 4)
        nbytes = (nbytes + 63) // 64 * 64
        off = self.cur
        self.cur += nbytes
        assert self.cur <= self.limit, ("SBUF overflow", name, self.cur)
        return self.P.sbuf_at(name, shape, dtype, off)

    def ring(self, name, n, shape, dtype):
        return [self.alloc("%s%d" % (name, i), shape, dtype) for i in range(n)]


C_ID = 0
C_MDA = (128, 256)
C_MASK = (384, 512)
C_SEL = (640, 644)
C_ONE = 648
NCONST = 776


def make_consts():
    c = np.zeros((128, NCONST), np.float32)
    c[:, C_ID:C_ID + 128] = np.eye(128, dtype=np.float32)
    s = np.arange(128)[:, None]
    t = np.arange(128)[None, :]
    same = (s // 64) == (t // 64)
    js, jt = s % 64, t % 64
    c[:, C_MDA[0]:C_MDA[0] + 128] = same * ((js <= jt).astype(np.float32) - (js <= 31).astype(np.float32))
    c[:, C_MDA[1]:C_MDA[1] + 128] = same * ((js >= jt).astype(np.float32) - (js >= 32).astype(np.float32))
    c[:, C_MASK[0]:C_MASK[0] + 128] = (same & (js <= jt)).astype(np.float32)
    c[:, C_MASK[1]:C_MASK[1] + 128] = (same & (js >= jt)).astype(np.float32)
    sv = np.arange(128)
    ch, j = sv // 64, sv % 64
    for cc in range(2):
        c[:, C_SEL[0] + cc] = (ch == cc) & (j <= 31)
        c[:, C_SEL[0] + 2 + cc] = (ch == cc) & (j >= 32)
        c[:, C_SEL[1] + cc] = (ch == cc) & (j >= 32)
        c[:, C_SEL[1] + 2 + cc] = (ch == cc) & (j <= 31)
    c[:, C_ONE:C_ONE + 128] = 1.0
    return c


def build(stop_after=None, dbg=()):
    nc = bass.Bass("TRN2", target_bir_lowering=False)
    P = Prog(nc)
    I = {}

    def din(name, shape):
        I[name] = P.dram(name, shape, F32, kind="ExternalInput")
        return I[name]

    x_in = din("x", [SEQ, D])
    ctx_in = din("ctx", [CTX, D])
    cc_in = din("cc", [128, 8, 2])
    ada_w = din("ada_w", [2, D, 6 * D])
    ada_b = din("ada_b", [2, 6 * D])
    norm1_w = din("norm1_w", [2, D])
    w_in = din("w_in", [2, D, DIN])
    sgu_ln_w = din("sgu_ln_w", [2, D])
    sgu_ln_b = din("sgu_ln_b", [2, D])
    sgu_wT = din("sgu_wT", [2, 128, 8, 128])
    sgu_b = din("sgu_b", [2, D])
    lbraw = din("hgrn_lower_bounds", [2, 2 * D])
    hnw = din("hgrn_norm_w", [2, 128])
    w_a = din("w_branch_a", [2, D, D])
    w_b = din("w_branch_b", [2, D, D])
    w_o = din("w_out", [2, D, D])
    norm2_w = din("norm2_w", [2, D])
    w_up = din("ffn_w_up", [2, D, 2 * DFF])
    convw = din("convw", [2, 128, NCC, 9])
    convb = din("convb", [2, 128, NCC])
    w_dn = din("ffn_w_down", [2, DFF, D])
    fnw = din("final_norm_w", [1, D])
    consts_in = din("consts", [128, NCONST])
    out = P.dram("out", [SEQ, D], F32, kind="ExternalOutput")

    dbgk = set(dbg)

    def scratch(name, shape, dtype):
        return P.dram(name, shape, dtype, kind="ExternalOutput" if name in dbgk else "Internal")

    WB_in = [scratch("WB_in%d" % l, [D, DIN], BF16) for l in range(2)]
    WB_a = [scratch("WB_a%d" % l, [D, D], BF16) for l in range(2)]
    WB_b = [scratch("WB_b%d" % l, [D, D], BF16) for l in range(2)]
    WB_o = [scratch("WB_o%d" % l, [D, D], BF16) for l in range(2)]
    WB_up = [scratch("WB_up%d" % l, [D, 2 * DFF], BF16) for l in range(2)]
    WB_dn = [scratch("WB_dn%d" % l, [DFF, D], BF16) for l in range(2)]
    modD = scratch("modD", [2, 2, 6 * D], F32)
    NB = {"x": NBX, "c": NBC}
    XA = {"x": x_in, "c": ctx_in}
    HT = {s: scratch("HT_" + s, [NB[s], 128, D], BF16) for s in "xc"}
    OD = {(s, d): scratch("O%s_%s" % (d, s), [NB[s], 128, D], F32) for s in "xc" for d in "fb"}
    MB = {s: scratch("MB_" + s, [NB[s], 128, D], F32) for s in "xc"}
    XMID = {s: scratch("XMID_" + s, [NB[s] * 128, D], F32) for s in "xc"}
    GA = {"x": scratch("GA_x", [8, 128, NCC, 512], BF16), "c": scratch("GA_c", [1, 128, NCC, 256], BF16)}
    XN = {"x": scratch("XN_x", [SEQ, D], F32), "c": scratch("XN_c", [CTX, D], F32)}

    PS = [P.psum("ps%d" % i, [128, 1024], F32) for i in range(4)]

    def psf(i, h=None):
        if h is None:
            return V(PS[i].t[:, :], ((PS[i].name, 0), (PS[i].name, 1)))
        return V(PS[i].t[:, h * 512:(h + 1) * 512], ((PS[i].name, h),))

    def psb(i, h):
        return V(PS[i].t[:, h * 512:(h + 1) * 512].bitcast(BF16).rearrange("p (c t) -> p c t", t=128),
                 ((PS[i].name, h),))

    def ps3(i):
        return V(PS[i].t[:, :].rearrange("p (c t) -> p c t", t=128), ((PS[i].name, 0), (PS[i].name, 1)))

    A0 = Arena(P, 16512)
    consts = A0.alloc("consts", [128, NCONST], F32)
    identb = A0.alloc("identb", [128, 128], BF16)
    A = Arena(P, A0.cur)

    def cst(off, n=128):
        return consts[:, off:off + n]

    P.dma(consts[:, :], consts_in[:, :])
    P.copy(identb[:, :], cst(C_ID), eng="dve")

    def cast_w(dst, src, rows, cols):
        for r0 in range(0, rows, 1024):
            r1 = min(rows, r0 + 1024)
            for c0 in range(0, cols, 1024):
                c1 = min(cols, c0 + 1024)
                P.dma(V(dst.t[r0:r1, c0:c1], ((dst.name, r0 // 1024, c0 // 1024),)), V(src[r0:r1, c0:c1], ()), eng="pool")

    for l in range(2):
        cast_w(WB_in[l], w_in.t[l], D, DIN)
        cast_w(WB_a[l], w_a.t[l], D, D)
        cast_w(WB_b[l], w_b.t[l], D, D)
        cast_w(WB_o[l], w_o.t[l], D, D)
        cast_w(WB_up[l], w_up.t[l], D, 2 * DFF)
        cast_w(WB_dn[l], w_dn.t[l], DFF, D)

    def wkeys(WB, c0, ncol, rows=D):
        return tuple((WB.name, r, c) for r in range((rows + 1023) // 1024) for c in range(c0 // 1024, (c0 + ncol - 1) // 1024 + 1))

    def load_w(dst, WB, c0, ncol, rows=D):
        src = WB.t[:, c0:c0 + ncol].rearrange("(c p) j -> p c j", p=128)
        P.dma(dst[:, :, :], V(src, wkeys(WB, c0, ncol, rows)))

    def bc_load(dst, src_ap):
        P.dma(dst, V(src_ap.partition_broadcast(128), ()))

    def phase_mod():
        A.reset()
        ccs = A.alloc("ccs", [128, 8, 2], F32)
        scc = A.alloc("scc", [128, 8, 2], F32)
        adab = A.alloc("adab", [1, 12 * D], F32)
        aw = A.ring("aw", 4, [128, 8, 512], F32)
        mrow = A.ring("mrow", 2, [2, 512], F32)
        P.dma(ccs[:, :, :], cc_in[:, :, :])
        P.act(scc[:, :, :], ccs[:, :, :], AF.Silu)
        P.dma(adab[0:1, 0:6 * D], V(ada_b.t[0:1, :], ()))
        P.dma(adab[0:1, 6 * D:12 * D], V(ada_b.t[1:2, :], ()))
        items = [(l, g) for l in range(2) for g in range(12)]

        def ld(n):
            l, g = items[n]
            src = ada_w.t[l, :, g * 512:(g + 1) * 512].rearrange("(c p) j -> p c j", p=128)
            P.dma(aw[n % 4][:, :, :], V(src, ()))

        for n in range(3):
            ld(n)
        for n, (l, g) in enumerate(items):
            if n + 3 < len(items):
                ld(n + 3)
            awt = aw[n % 4]
            mr = mrow[n % 2]
            po = V(PS[n % 2].t[0:2, 0:512], ((PS[n % 2].name, 0),))
            for kc in range(8):
                P.mm(po, scc[:, kc, :], awt[:, kc, :], start=(kc == 0), stop=False)
            P.mm(po, V(consts.t[0:1, C_ONE:C_ONE + 2], consts[:, :].keys),
                 adab[0:1, l * 6 * D + g * 512:l * 6 * D + (g + 1) * 512], start=False, stop=True)
            P.copy(mr[:, :], po, eng="act")
            P.dma(V(modD.t[l, :, g * 512:(g + 1) * 512], ((modD.name, l, g),)), mr[:, :], eng="act")

    def mod_bc(dst, l, s, j):
        si = 0 if s == "x" else 1
        P.dma(dst, V(modD.t[l, si, j * D:(j + 1) * D].partition_broadcast(128), ((modD.name, l, 2 * j), (modD.name, l, 2 * j + 1))))

    def run_staged(stages, NBLK):
        maxd = max(dl for dl, _ in stages)
        for it in range(NBLK + maxd):
            for dl, fn in stages:
                n = it - dl
                if 0 <= n < NBLK:
                    fn(n)

    def phase_norm(l, which, SRC, streams):
        A.reset()
        nw_in = norm1_w if which == 1 else norm2_w
        nwb = A.alloc("nwb", [128, D], F32)
        bc_load(nwb[:, :], nw_in.t[l, :])
        Abc = {}
        SHbc = {}
        for s in streams:
            Abc[s] = A.alloc("Abc" + s, [128, D], F32)
            SHbc[s] = A.alloc("SHbc" + s, [128, D], F32)
            mod_bc(Abc[s][:, :], l, s, 1 if which == 1 else 4)
            mod_bc(SHbc[s][:, :], l, s, 0 if which == 1 else 3)
            P.stt(Abc[s][:, :], Abc[s][:, :], 1.0, nwb[:, :], ALU.add, ALU.mult)
        xr = A.ring("xr", 4, [128, D], F32)
        junk = A.alloc("junk", [128, D], BF16)
        st = A.ring("st", 3, [128, 4], F32)
        hf = A.ring("hf", 2, [128, D], F32)
        hb = A.ring("hb", 2, [128, D], BF16)
        hT = A.ring("hT", 3, [128, 8, 128], BF16)
        blocks = [(s, j) for s in streams for j in range(NB[s])]

        def load(n):
            s, j = blocks[n]
            P.dma(xr[n % 4][:, :], V(SRC[s].t[j * 128:(j + 1) * 128, :], ((SRC[s].name, j),)))

        def N0(n):
            x_ = xr[n % 4]
            r = n % 3
            P.act(junk[:, :], x_[:, :], AF.Square, accum_out=st[r][:, 0:1])
            P.ts(st[r][:, 1:2], st[r][:, 0:1], 1.0 / D, ALU.mult, EPS, ALU.add)
            P.act(st[r][:, 2:3], st[r][:, 1:2], AF.Sqrt)
            P.recip(st[r][:, 3:4], st[r][:, 2:3])

        def N1(n):
            s, j = blocks[n]
            x_ = xr[n % 4]
            r = n % 2
            P.stt(hf[r][:, :], x_[:, :], st[n % 3][:, 3:4], Abc[s][:, :], ALU.mult, ALU.mult)
            P.tt(hb[r][:, :], hf[r][:, :], SHbc[s][:, :], ALU.add, eng="pool")

        def N2(n):
            s, j = blocks[n]
            r = n % 2
            pt = psb(2 + (n % 2), 0)
            for c in range(8):
                P.tr(V(pt.ap[:, c, :], pt.keys), hb[r][:, c * 128:(c + 1) * 128], identb[:, :])
            P.copy(hT[n % 3][:, :, :], pt, eng="act")
            P.dma(V(HT[s].t[j], ((HT[s].name, j),)),
                  V(hT[n % 3].t[:, :, :].rearrange("p c t -> p (c t)"), hT[n % 3][:, :, :].keys), eng="act")

        run_staged([(3, N2), (2, N1), (1, N0), (0, load)], len(blocks))

    def phase_gla(l, d):
        A.reset()
        di = 0 if d == "f" else 1
        Wq = A.alloc("Wq", [128, 8, D], BF16)
        Wf = A.alloc("Wf", [128, 8, D], BF16)
        Wi = A.alloc("Wi", [128, 8, D], BF16)
        load_w(Wq, WB_in[l], 0, D)
        load_w(Wf, WB_in[l], (1 + di) * D, D)
        load_w(Wi, WB_in[l], 3 * D, D)
        oml = A.alloc("oml", [128, D], F32)
        if l == 0:
            P.memset(oml[:, :], 1.0)
        else:
            t0 = A.alloc("lb0", [128, D], F32)
            bc_load(t0[:, :], lbraw.t[0, di * D:(di + 1) * D])
            bc_load(oml[:, :], lbraw.t[1, di * D:(di + 1) * D])
            P.tt(oml[:, :], t0[:, :], oml[:, :], ALU.subtract)
            P.act(oml[:, :], oml[:, :], AF.Sigmoid)
        Vst = A.alloc("Vst", [128, 8, 128], F32)
        P.memset(Vst[:, :, :], 0.0)
        ones8 = A.alloc("ones8", [128, 8], F32)
        P.memset(ones8[:, :], 1.0)
        hT = A.ring("hT", 3, [128, 8, 128], BF16)
        qs = A.alloc("qs", [128, D], F32)
        sg = A.alloc("sg", [128, D], F32)
        kk = A.alloc("kk", [128, D], F32)
        gg = A.alloc("gg", [128, D], F32)
        ib = A.ring("ib", 2, [128, D], BF16)
        E1 = A.alloc("E1", [128, D], F32)
        E1n = A.alloc("E1n", [128, D], F32)
        qtil = A.alloc("qtil", [128, D], BF16)
        ktil = A.ring("ktil", 2, [128, D], BF16)
        ev = A.ring("ev", 3, [128, 8, 4], F32)
        qT = A.ring("qT", 2, [128, 8, 128], BF16)
        kT = A.ring("kT", 2, [128, 8, 128], BF16)
        scm = A.ring("scm", 2, [128, 8, 128], BF16)
        cvec = A.ring("cvec", 2, [128, 8], F32)
        Ub = A.ring("Ub", 2, [128, 8, 128], BF16)
        osb = A.ring("osb", 2, [128, D], F32)
        mdab = A.alloc("mdab", [128, 128], BF16)
        cselb = A.alloc("cselb", [128, 4], BF16)
        P.copy(mdab[:, :], cst(C_MDA[di]), eng="dve")
        P.copy(cselb[:, :], cst(C_SEL[di], 4), eng="dve")
        ghi = A.alloc("ghi", [128, D], BF16)
        glo = A.alloc("glo", [128, D], BF16)
        mask = V(consts.t[:, C_MASK[di]:C_MASK[di] + 128].bitcast(mybir.dt.uint32).unsqueeze(1).to_broadcast([128, 8, 128]),
                 consts[:, :].keys)
        for b_ in scm:
            P.memset(b_[:, :, :], 0.0)
        csel = cst(C_SEL[di], 4)
        blocks = [("c", j) for j in range(NBC)] + [("x", j) for j in range(NBX)]
        if d == "b":
            blocks = [("c", j) for j in reversed(range(NBC))] + [("x", j) for j in reversed(range(NBX))]
        corder = (0, 1) if d == "f" else (1, 0)
        NBLK = len(blocks)
        state = {"prev_ebr": ones8[:, :]}

        def load(n):
            s, j = blocks[n]
            P.dma(V(hT[n % 3].t[:, :, :].rearrange("p c t -> p (c t)"), hT[n % 3][:, :, :].keys),
                  V(HT[s].t[j], ((HT[s].name, j),)))

        def proj(n, W, pi):
            h = hT[n % 3]
            for hh in range(2):
                for kc in range(8):
                    P.mm(psf(pi, hh), h[:, kc, :], W[:, kc, hh * 512:(hh + 1) * 512], start=(kc == 0), stop=(kc == 7))

        def A1(n):
            proj(n, Wf, 1)
            P.act(sg[:, :], psf(1), AF.Sigmoid, scale=-1.0)
            P.tt(kk[:, :], sg[:, :], oml[:, :], ALU.mult)
            P.act(gg[:, :], kk[:, :], AF.Ln, scale=-1.0, bias=1.0)
            P.copy(ghi[:, :], gg[:, :], eng="pool")
            P.tt(glo[:, :], gg[:, :], ghi[:, :], ALU.subtract, eng="pool")

        def A2(n):
            proj(n, Wq, 0)
            P.act(qs[:, :], psf(0), AF.Silu)

        def A3(n):
            proj(n, Wi, 1)
            P.copy(ib[n % 2][:, :], psf(1), eng="dve")

        def A4(n):
            r = n % 2
            for hh in range(2):
                P.mm(psf(0, hh), mdab[:, :], ghi[:, hh * 512:(hh + 1) * 512], start=True, stop=False)
                P.mm(psf(0, hh), mdab[:, :], glo[:, hh * 512:(hh + 1) * 512], start=False, stop=True)
            pv = V(PS[1].t[:, 0:32].rearrange("p (h c) -> p h c", c=4), ((PS[1].name, 0),))
            for hd in range(8):
                P.mm(V(pv.ap[:, hd, :], pv.keys), ghi[:, hd * 128:(hd + 1) * 128], cselb[:, :], start=True, stop=False)
                P.mm(V(pv.ap[:, hd, :], pv.keys), glo[:, hd * 128:(hd + 1) * 128], cselb[:, :], start=False, stop=True)
            e = ev[n % 3]
            P.act(e[:, :, :], pv, AF.Exp)
            P.act(E1[:, :], psf(0), AF.Exp)
            P.act(E1n[:, :], psf(0), AF.Exp, scale=-1.0)
            P.tt(qtil[:, :], qs[:, :], E1[:, :], ALU.mult)
            P.tt(ktil[r][:, :], kk[:, :], E1n[:, :], ALU.mult, eng="pool")

        def A5(n):
            r = n % 2
            ptq = psb(1, 1)
            for c in range(8):
                P.tr(V(ptq.ap[:, c, :], ptq.keys), qtil[:, c * 128:(c + 1) * 128], identb[:, :])
            P.copy(qT[r][:, :, :], ptq, eng="act")
            ptk = psb(0, 0)
            for c in range(8):
                P.tr(V(ptk.ap[:, c, :], ptk.keys), ktil[r][:, c * 128:(c + 1) * 128], identb[:, :])
            P.copy(kT[r][:, :, :], ptk, eng="dve")

        def B1(n):
            r = n % 2
            sc = ps3(2)
            for hd in range(8):
                P.mm(V(sc.ap[:, hd, :], ((PS[2].name, hd // 4),)), kT[r][:, hd, :], qT[r][:, hd, :])
            P.cpred(scm[r][:, :, :], mask, sc)

        def Bpre(n, ci):
            e = ev[n % 3]
            cv = cvec[ci]
            P.tt(cv[:, :], state["prev_ebr"], e[:, :, ci], ALU.mult)
            P.tt(Vst[:, :, :], Vst[:, :, :],
                 V(cv.t[:, :].unsqueeze(2).to_broadcast([128, 8, 128]), cv[:, :].keys), ALU.mult)
            P.copy(Ub[ci][:, :, :], Vst[:, :, :], eng="act")

        def Bmm(n, ci):
            r = n % 2
            e = ev[n % 3]
            lo, hi = 64 * ci, 64 * ci + 64
            kv = ps3(2)
            for hd in range(8):
                P.mm(V(kv.ap[:, hd, :], ((PS[2].name, hd // 4),)),
                     ktil[r][lo:hi, hd * 128:(hd + 1) * 128], ib[r][lo:hi, hd * 128:(hd + 1) * 128])
            for hd in range(8):
                oo = V(PS[3].t[lo:hi, hd * 128:(hd + 1) * 128], ((PS[3].name, hd // 4),))
                P.mm(oo, scm[r][:, hd, lo:hi], ib[r][:, hd * 128:(hd + 1) * 128], start=True, stop=False)
                P.mm(oo, qT[r][:, hd, lo:hi], Ub[ci][:, hd, :], start=False, stop=True)
            P.tt(Vst[:, :, :], Vst[:, :, :], kv, ALU.add)
            state["prev_ebr"] = e[:, :, 2 + ci]

        def B3(n):
            r = n % 2
            s, j = blocks[n]
            P.copy(osb[r][:, :], psf(3), eng="act")
            P.dma(V(OD[(s, d)].t[j], ((OD[(s, d)].name, j),)), osb[r][:, :], eng="act")

        load(0)
        load(1)
        for st_ in (A1, A2, A3, A4):
            st_(0)
        for n in range(NBLK):
            nx = n + 1 if n + 1 < NBLK else None
            if n + 2 < NBLK:
                load(n + 2)
            if nx is not None:
                A1(nx)
            A5(n)
            B1(n)
            Bpre(n, corder[0])
            if nx is not None:
                A2(nx)
            Bmm(n, corder[0])
            Bpre(n, corder[1])
            if nx is not None:
                A3(nx)
            Bmm(n, corder[1])
            if nx is not None:
                A4(nx)
            B3(n)

    def phase_read(l, streams):
        A.reset()
        Wog = A.alloc("Wog", [128, 8, D], BF16)
        Wgb = A.alloc("Wgb", [128, 8, D], BF16)
        Wb = A.alloc("Wb", [128, 8, D], BF16)
        load_w(Wog, WB_in[l], 6 * D, D)
        load_w(Wgb, WB_in[l], 8 * D, D)
        load_w(Wb, WB_b[l], 0, D)
        hncol = A.alloc("hncol", [128, 1], F32)
        P.dma(hncol[:, :], V(hnw.t[l, :].rearrange("(p o) -> p o", o=1), ()))
        for kc in range(8):
            P.act(Wb[:, kc, :], Wb[:, kc, :], AF.Copy, scale=hncol[:, 0:1])
        hT = A.ring("hT", 3, [128, 8, 128], BF16)
        of = A.ring("of", 3, [128, D], F32)
        ob = A.ring("ob", 3, [128, D], F32)
        o = A.ring("o", 4, [128, D], F32)
        o2 = A.alloc("o2", [128, D], F32)
        st = A.ring("st", 4, [128, 8, 4], F32)
        on = A.alloc("on", [128, D], F32)
        sog = A.ring("sog", 5, [128, D], F32)
        yb = A.ring("yb", 3, [128, D], BF16)
        ybT = A.ring("ybT", 3, [128, 8, 128], BF16)
        sgb = A.ring("sgb", 10, [128, D], F32)
        mb = A.ring("mb", 2, [128, D], F32)
        blocks = [(s, j) for s in streams for j in range(NB[s])]
        NBLK = len(blocks)

        def Lh(n):
            s, j = blocks[n]
            r = n % 3
            P.dma(V(hT[r].t[:, :, :].rearrange("p c t -> p (c t)"), hT[r][:, :, :].keys), V(HT[s].t[j], ((HT[s].name, j),)))
            P.dma(of[r][:, :], V(OD[(s, "f")].t[j], ((OD[(s, "f")].name, j),)))
            P.dma(ob[r][:, :], V(OD[(s, "b")].t[j], ((OD[(s, "b")].name, j),)))

        def SA(n):
            h = hT[n % 3]
            for W, pi in ((Wog, 0), (Wgb, 1)):
                for hh in range(2):
                    for kc in range(8):
                        P.mm(psf(pi, hh), h[:, kc, :], W[:, kc, hh * 512:(hh + 1) * 512], start=(kc == 0), stop=(kc == 7))
            P.act(sog[n % 5][:, :], psf(0), AF.Silu)
            P.act(sgb[n % 10][:, :], psf(1), AF.Sigmoid)

        def SB(n):
            o_ = o[n % 4]
            s_ = st[n % 4]
            P.tt(o_[:, :], of[n % 3][:, :], ob[n % 3][:, :], ALU.add)
            P.tt(o2[:, :], o_[:, :], o_[:, :], ALU.mult)
            P.add("dve", lambda e: e.tensor_reduce(s_.t[:, :, 0], o2.t[:, :].rearrange("p (h v) -> p h v", v=128), AX.X, ALU.add),
                  [o2[:, :]], [s_[:, :, :]])
            P.ts(s_[:, :, 1], s_[:, :, 0], 1.0 / 128, ALU.mult, EPS, ALU.add)

        def SB2(n):
            s_ = st[n % 4]
            P.act(s_[:, :, 2], s_[:, :, 1], AF.Sqrt)

        def SB3(n):
            o_ = o[n % 4]
            s_ = st[n % 4]
            P.recip(s_[:, :, 3], s_[:, :, 2])
            o3 = V(o_.t[:, :].rearrange("p (h v) -> p h v", v=128), o_[:, :].keys)
            on3 = V(on.t[:, :].rearrange("p (h v) -> p h v", v=128), on[:, :].keys)
            P.tt(on3, o3, V(s_.t[:, :, 3:4].to_broadcast([128, 8, 128]), s_[:, :, :].keys), ALU.mult)
            P.tt(yb[n % 3][:, :], on[:, :], sog[n % 5][:, :], ALU.mult)

        def SC(n):
            pt = psb(2, 0)
            for c in range(8):
                P.tr(V(pt.ap[:, c, :], pt.keys), yb[n % 3][:, c * 128:(c + 1) * 128], identb[:, :])

        def SD(n):
            P.copy(ybT[n % 3][:, :, :], psb(2, 0), eng="act")

        def SE(n):
            for hh in range(2):
                for kc in range(8):
                    P.mm(psf(3, hh), ybT[n % 3][:, kc, :], Wb[:, kc, hh * 512:(hh + 1) * 512], start=(kc == 0), stop=(kc == 7))

        def SF(n):
            s, j = blocks[n]
            r = n % 2
            P.tt(mb[r][:, :], psf(3), sgb[n % 10][:, :], ALU.mult)
            P.dma(V(MB[s].t[j], ((MB[s].name, j),)), mb[r][:, :])

        run_staged([(8, SF), (6, SD), (0, Lh), (3, SB2), (2, SB), (4, SB3), (7, SE), (5, SC), (1, SA)], NBLK)

    def phase_sgu(l, streams):
        A.reset()
        Wu = A.alloc("Wu", [128, 8, D], BF16)
        Wv = A.alloc("Wv", [128, 8, D], BF16)
        Wga = A.alloc("Wga", [128, 8, D], BF16)
        Wa = A.alloc("Wa", [128, 8, D], BF16)
        Wo = A.alloc("Wo", [128, 8, D], BF16)
        load_w(Wu, WB_in[l], 4 * D, D)
        load_w(Wv, WB_in[l], 5 * D, D)
        load_w(Wga, WB_in[l], 7 * D, D)
        load_w(Wa, WB_a[l], 0, D)
        load_w(Wo, WB_o[l], 0, D)
        swf = A.alloc("swf", [128, 8, 128], F32)
        swT = A.alloc("swT", [128, 8, 128], BF16)
        P.dma(swf[:, :, :], V(sgu_wT.t[l], ()))
        P.copy(swT[:, :, :], swf[:, :, :], eng="pool")
        bsf = A.alloc("bsf", [1, D], F32)
        bsh = A.alloc("bsh", [1, D], BF16)
        bsr = A.alloc("bsr", [1, D], F32)
        bsl = A.alloc("bsl", [1, D], BF16)
        oneb = A.alloc("oneb", [1, 128], BF16)
        P.dma(bsf[:, :], V(sgu_b.t[l:l + 1, :], ()))
        P.copy(bsh[:, :], bsf[:, :], eng="dve")
        P.tt(bsr[:, :], bsf[:, :], bsh[:, :], ALU.subtract)
        P.copy(bsl[:, :], bsr[:, :], eng="dve")
        P.memset(oneb[:, :], 1.0)
        lnw = A.alloc("lnw", [128, D], F32)
        lnb = A.alloc("lnb", [128, D], F32)
        bc_load(lnw[:, :], sgu_ln_w.t[l, :])
        bc_load(lnb[:, :], sgu_ln_b.t[l, :])
        G1 = {}
        for s in streams:
            G1[s] = A.alloc("G1" + s, [128, D], F32)
            mod_bc(G1[s][:, :], l, s, 2)
        hT = A.ring("hT", 4, [128, 8, 128], BF16)
        mbl = A.ring("mbl", 2, [128, D], F32)
        xr = A.ring("xr", 2, [128, D], F32)
        gv = A.ring("gv", 2, [128, D], F32)
        junk = A.alloc("junk", [128, D], BF16)
        st = A.ring("st", 3, [128, 8], F32)
        vn = A.ring("vn", 2, [128, D], BF16)
        t1 = A.alloc("t1", [128, D], F32)
        t2 = A.alloc("t2", [128, D], F32)
        t3 = A.alloc("t3", [128, D], F32)
        guT = A.ring("guT", 3, [128, 8, 128], BF16)
        yaT = A.ring("yaT", 2, [128, 8, 128], BF16)
        sga = A.ring("sga", 2, [128, D], F32)
        mg = A.ring("mg", 2, [128, D], BF16)
        mgT = A.ring("mgT", 2, [128, 8, 128], BF16)
        xo = A.ring("xo", 2, [128, D], F32)
        blocks = [(s, j) for s in streams for j in range(NB[s])]
        NBLK = len(blocks)

        def Lh(n):
            s, j = blocks[n]
            r = n % 4
            P.dma(V(hT[r].t[:, :, :].rearrange("p c t -> p (c t)"), hT[r][:, :, :].keys), V(HT[s].t[j], ((HT[s].name, j),)))

        def Lm(n):
            s, j = blocks[n]
            P.dma(mbl[n % 2][:, :], V(MB[s].t[j], ((MB[s].name, j),)))

        def Lx(n):
            s, j = blocks[n]
            P.dma(xr[n % 2][:, :], V(XA[s].t[j * 128:(j + 1) * 128, :], ((XA[s].name, j),)))

        def S0a(n):
            h = hT[n % 4]
            r = n % 2
            for hh in range(2):
                for kc in range(8):
                    P.mm(psf(0, hh), h[:, kc, :], Wv[:, kc, hh * 512:(hh + 1) * 512], start=(kc == 0), stop=(kc == 7))
            s_ = st[n % 3]
            P.act(gv[r][:, :], psf(0), AF.Gelu_apprx_tanh, accum_out=s_[:, 0:1])
            P.act(junk[:, :], gv[r][:, :], AF.Square, accum_out=s_[:, 1:2])

        def S0b(n):
            h = hT[n % 4]
            pu = ps3(1)
            for c in range(8):
                for kc in range(8):
                    P.mm(V(pu.ap[:, c, :], ((PS[1].name, c // 4),)), Wu[:, kc, c * 128:(c + 1) * 128], h[:, kc, :],
                         start=(kc == 0), stop=(kc == 7))
            P.act(guT[n % 3][:, :, :], pu, AF.Gelu_apprx_tanh)

        def S1(n):
            r = n % 2
            s_ = st[n % 3]
            P.ts(s_[:, 2:3], s_[:, 0:1], 1.0 / D, ALU.mult)
            P.ts(s_[:, 3:4], s_[:, 1:2], 1.0 / D, ALU.mult, LN_EPS, ALU.add)
            P.tt(s_[:, 4:5], s_[:, 2:3], s_[:, 2:3], ALU.mult)
            P.tt(s_[:, 5:6], s_[:, 3:4], s_[:, 4:5], ALU.subtract)
            P.act(s_[:, 6:7], s_[:, 5:6], AF.Sqrt)
            P.recip(s_[:, 7:8], s_[:, 6:7])
            P.ts(t1[:, :], gv[r][:, :], s_[:, 2:3], ALU.subtract, s_[:, 7:8], ALU.mult)
            P.tt(t1[:, :], t1[:, :], lnw[:, :], ALU.mult, eng="pool")
            P.tt(vn[r][:, :], t1[:, :], lnb[:, :], ALU.add, eng="pool")

        def S2(n):
            r = n % 2
            h = hT[n % 4]
            pm = ps3(2)
            for g in range(8):
                og_ = V(pm.ap[:, g, :], ((PS[2].name, g // 4),))
                P.mm(og_, vn[r][:, g * 128:(g + 1) * 128], swT[:, g, :], start=True, stop=False)
                P.mm(og_, oneb[0:1, :], bsh[0:1, g * 128:(g + 1) * 128], start=False, stop=False)
                P.mm(og_, oneb[0:1, :], bsl[0:1, g * 128:(g + 1) * 128], start=False, stop=True)
            P.tt(yaT[r][:, :, :], pm, guT[n % 3][:, :, :], ALU.mult)
            for hh in range(2):
                for kc in range(8):
                    P.mm(psf(0, hh), h[:, kc, :], Wga[:, kc, hh * 512:(hh + 1) * 512], start=(kc == 0), stop=(kc == 7))
            P.act(sga[r][:, :], psf(0), AF.Sigmoid)

        def S3(n):
            r = n % 2
            for hh in range(2):
                for kc in range(8):
                    P.mm(psf(3, hh), yaT[r][:, kc, :], Wa[:, kc, hh * 512:(hh + 1) * 512], start=(kc == 0), stop=(kc == 7))
            P.tt(t2[:, :], psf(3), sga[r][:, :], ALU.mult)
            P.tt(mg[r][:, :], t2[:, :], mbl[r][:, :], ALU.add, eng="pool")

        def S4(n):
            r = n % 2
            pt = psb(2, 0)
            for c in range(8):
                P.tr(V(pt.ap[:, c, :], pt.keys), mg[r][:, c * 128:(c + 1) * 128], identb[:, :])
            P.copy(mgT[r][:, :, :], pt, eng="act")

        def S5(n):
            s, j = blocks[n]
            r = n % 2
            for hh in range(2):
                for kc in range(8):
                    P.mm(psf(3, hh), mgT[r][:, kc, :], Wo[:, kc, hh * 512:(hh + 1) * 512], start=(kc == 0), stop=(kc == 7))
            P.tt(t3[:, :], psf(3), G1[s][:, :], ALU.mult)
            P.tt(xo[r][:, :], t3[:, :], xr[r][:, :], ALU.add, eng="pool")
            P.dma(V(XMID[s].t[j * 128:(j + 1) * 128, :], ((XMID[s].name, j),)), xo[r][:, :])

        run_staged([(0, Lh), (3, Lm), (5, Lx), (2, S1), (4, S3), (1, S0a), (5, S4), (1, S0b), (6, S5), (3, S2)], NBLK)

    def phase_ffn1(l, streams):
        A.reset()
        Wua = A.alloc("Wua", [128, 8, DFF], BF16)
        load_w(Wua, WB_up[l], 0, DFF)
        cwf = A.alloc("cwf", [128, NCC, 9], F32)
        cbf = A.alloc("cbf", [128, NCC], F32)
        P.dma(cwf[:, :, :], V(convw.t[l], ()))
        P.dma(cbf[:, :], V(convb.t[l], ()))
        dg = A.alloc("dg", [128, NCC, 9, 128], BF16)
        idb = V(consts.t[:, C_ID:C_ID + 128].unsqueeze(1).to_broadcast([128, NCC, 128]), consts[:, :].keys)
        for tp in range(9):
            P.tt(dg[:, :, tp, :], idb, V(cwf.t[:, :, tp:tp + 1].to_broadcast([128, NCC, 128]), cwf[:, :, :].keys),
                 ALU.mult, eng=("dve" if tp % 2 == 0 else "pool"))
        h2 = A.ring("h2", 2, [128, 8, 640], BF16)
        apx = A.ring("apx", 3, [128, 10, 66], BF16)
        apc = A.ring("apc", 3, [128, 258], BF16)
        for b_ in apx:
            P.memset(b_[:, :, :], 0.0)
        for b_ in apc:
            P.memset(b_[:, :], 0.0)
        gat = A.ring("gat", 2, [128, NCC, 512], BF16)
        tiles = []
        for s in streams:
            tiles += [(s, t) for t in range(8)] if s == "x" else [("c", 0)]

        def loads(n):
            s, t = tiles[n]
            hh = h2[n % 2]
            if s == "x":
                for b in range(-1, 5):
                    j = 4 * t + b
                    if j < 0 or j >= NBX:
                        continue
                    src = HT[s].t[j].rearrange("p (c t) -> p c t", t=128)
                    if b == -1:
                        P.dma(hh[:, :, 0:64], V(src[:, :, 64:128], ((HT[s].name, j),)))
                    elif b == 4:
                        P.dma(hh[:, :, 576:640], V(src[:, :, 0:64], ((HT[s].name, j),)))
                    else:
                        P.dma(hh[:, :, 64 + 128 * b:64 + 128 * (b + 1)], V(src, ((HT[s].name, j),)))
            else:
                for j in range(NBC):
                    src = HT[s].t[j].rearrange("p (c t) -> p c t", t=128)
                    P.dma(hh[:, :, 128 * j:128 * (j + 1)], V(src, ((HT[s].name, j),)))

        items = [(n, cc) for n in range(len(tiles)) for cc in range(NCC)]

        def stA(i):
            n, cc = items[i]
            s, t = tiles[n]
            if cc == 0:
                if n == 0:
                    loads(0)
                if n + 1 < len(tiles):
                    loads(n + 1)
            hh = h2[n % 2]
            wsl = lambda kc: Wua[:, kc, cc * 128:(cc + 1) * 128]
            if s == "x":
                ap_ = apx[i % 3]
                pa = psf(i % 2, 0)
                ph = psf(i % 2, 1)
                for kc in range(8):
                    P.mm(pa, wsl(kc), hh[:, kc, 64:576], start=(kc == 0), stop=(kc == 7))
                P.copy(ap_[:, 1:9, 1:65], V(pa.ap.rearrange("p (r w) -> p r w", w=64), pa.keys), eng="act")
                if t > 0:
                    for kc in range(8):
                        P.mm(V(ph.ap[:, 0:64], ph.keys), wsl(kc), hh[:, kc, 0:64], start=(kc == 0), stop=(kc == 7))
                    P.copy(ap_[:, 0, 1:65], V(ph.ap[:, 0:64], ph.keys), eng="dve")
                else:
                    P.memset(ap_[:, 0, 1:65], 0.0)
                if t < 7:
                    for kc in range(8):
                        P.mm(V(ph.ap[:, 64:128], ph.keys), wsl(kc), hh[:, kc, 576:640], start=(kc == 0), stop=(kc == 7))
                    P.copy(ap_[:, 9, 1:65], V(ph.ap[:, 64:128], ph.keys), eng="dve")
                else:
                    P.memset(ap_[:, 9, 1:65], 0.0)
            else:
                ap_ = apc[i % 3]
                pa = V(PS[i % 2].t[:, 0:256], ((PS[i % 2].name, 0),))
                for kc in range(8):
                    P.mm(pa, wsl(kc), hh[:, kc, 0:256], start=(kc == 0), stop=(kc == 7))
                P.copy(ap_[:, 1:257], pa, eng="act")

        def stB(i):
            n, cc = items[i]
            s, t = tiles[n]
            g_ = gat[n % 2]
            if s == "x":
                ap_ = apx[i % 3]
                pc = psf(2 + i % 2, 0)
                pc3 = V(pc.ap.rearrange("p (r w) -> p r w", w=64), pc.keys)
                for tp in range(9):
                    dy, dx = tp // 3, tp % 3
                    P.mm(pc3, dg[:, cc, tp, :], ap_[:, dy:dy + 8, dx:dx + 64], start=(tp == 0), stop=(tp == 8))
                P.act(g_[:, cc, :], pc, AF.Gelu_apprx_tanh, bias=cbf[:, cc:cc + 1])
            else:
                ap_ = apc[i % 3]
                pc = V(PS[2 + i % 2].t[:, 0:256], ((PS[2 + i % 2].name, 0),))
                for dx in range(3):
                    P.mm(pc, dg[:, cc, 3 + dx, :], ap_[:, dx:dx + 256], start=(dx == 0), stop=(dx == 2))
                P.act(g_[:, cc, 0:256], pc, AF.Gelu_apprx_tanh, bias=cbf[:, cc:cc + 1])
            if cc == NCC - 1:
                if s == "x":
                    P.dma(V(GA[s].t[t], ((GA[s].name, t),)), g_[:, :, :], eng="act")
                else:
                    P.dma(V(GA[s].t[0], ((GA[s].name, 0),)), g_[:, :, 0:256], eng="act")

        run_staged([(2, stB), (0, stA)], len(items))

    def phase_ffn2(l, streams, last):
        A.reset()
        Wuv = A.alloc("Wuv", [128, 8, DFF], BF16)
        Wd = A.alloc("Wd", [128, NCC, D], BF16)
        load_w(Wuv, WB_up[l], DFF, DFF)
        P.dma(Wd[:, :, :], V(WB_dn[l].t[:, :].rearrange("(c p) j -> p c j", p=128), wkeys(WB_dn[l], 0, D, DFF)))
        G2 = {}
        for s in streams:
            G2[s] = A.alloc("G2" + s, [128, D], F32)
            mod_bc(G2[s][:, :], l, s, 5)
        if last:
            fnb = A.alloc("fnb", [128, D], F32)
            bc_load(fnb[:, :], fnw.t[0, :])
        h2 = A.ring("h2", 2, [128, 8, 512], BF16)
        gal = A.ring("gal", 2, [128, NCC, 512], BF16)
        gvv = A.alloc("gvv", [128, NCC, 512], BF16)
        xr = A.ring("xr", 2, [128, D], F32)
        tmp = A.alloc("tmp", [128, D], F32)
        xo = A.ring("xo", 2, [128, D], F32)
        junk = A.alloc("junk", [128, D], BF16)
        st = A.ring("st", 2, [128, 4], F32)
        tiles = []
        for s in streams:
            tiles += [(s, t) for t in range(8)] if s == "x" else [("c", 0)]

        def loads(n):
            s, t = tiles[n]
            nb = 4 if s == "x" else 2
            ntok = nb * 128
            for b in range(nb):
                j = 4 * t + b
                src = HT[s].t[j].rearrange("p (c t) -> p c t", t=128)
                P.dma(h2[n % 2][:, :, 128 * b:128 * (b + 1)], V(src, ((HT[s].name, j),)))
            P.dma(gal[n % 2][:, :, 0:ntok], V(GA[s].t[t], ((GA[s].name, t),)))

        nblk = 0
        for n, (s, t) in enumerate(tiles):
            if n == 0:
                loads(0)
            if n + 1 < len(tiles):
                loads(n + 1)
            nb = 4 if s == "x" else 2
            ntok = nb * 128
            hh = h2[n % 2]
            for cc in range(NCC):
                pv = V(PS[cc % 2].t[:, 0:ntok], ((PS[cc % 2].name, 0),))
                for kc in range(8):
                    P.mm(pv, Wuv[:, kc, cc * 128:(cc + 1) * 128], hh[:, kc, 0:ntok], start=(kc == 0), stop=(kc == 7))
                P.tt(gvv[:, cc, 0:ntok], pv, gal[n % 2][:, cc, 0:ntok], ALU.mult)
            for b in range(nb):
                j = 4 * t + b
                r = nblk % 2
                nblk += 1
                P.dma(xr[r][:, :], V(XMID[s].t[j * 128:(j + 1) * 128, :], ((XMID[s].name, j),)))
                py = 2 + (nblk % 2)
                for hf in range(2):
                    for cc in range(NCC):
                        P.mm(psf(py, hf), gvv[:, cc, b * 128:(b + 1) * 128], Wd[:, cc, hf * 512:(hf + 1) * 512],
                             start=(cc == 0), stop=(cc == NCC - 1))
                P.tt(tmp[:, :], psf(py), G2[s][:, :], ALU.mult)
                if not last:
                    P.tt(xo[r][:, :], tmp[:, :], xr[r][:, :], ALU.add, eng="pool")
                    P.dma(V(XN[s].t[j * 128:(j + 1) * 128, :], ((XN[s].name, j),)), xo[r][:, :])
                else:
                    P.tt(tmp[:, :], tmp[:, :], xr[r][:, :], ALU.add, eng="pool")
                    s_ = st[r]
                    P.act(junk[:, :], tmp[:, :], AF.Square, accum_out=s_[:, 0:1])
                    P.ts(s_[:, 1:2], s_[:, 0:1], 1.0 / D, ALU.mult, EPS, ALU.add)
                    P.act(s_[:, 2:3], s_[:, 1:2], AF.Sqrt)
                    P.recip(s_[:, 3:4], s_[:, 2:3])
                    P.stt(xo[r][:, :], tmp[:, :], s_[:, 3:4], fnb[:, :], ALU.mult, ALU.mult)
                    P.dma(V(out.t[j * 128:(j + 1) * 128, :], ((out.name, j),)), xo[r][:, :])

    def done(tag):
        return stop_after == tag

    def run():
        phase_mod()
        P.barrier()
        if done("mod"):
            return
        for l in range(2):
            last = l == 1
            global_streams = "xc"
            if l == 1:
                XA["x"] = XN["x"]
                XA["c"] = XN["c"]
            phase_norm(l, 1, XA, "cx")
            P.barrier()
            if done("n1_%d" % l):
                return
            phase_gla(l, "b")
            P.barrier()
            if done("gb_%d" % l):
                return
            phase_gla(l, "f")
            P.barrier()
            if done("gf_%d" % l):
                return
            streams = "x" if last else "xc"
            phase_read(l, streams)
            P.barrier()
            if done("rd_%d" % l):
                return
            phase_sgu(l, streams)
            P.barrier()
            if done("sg_%d" % l):
                return
            phase_norm(l, 2, XMID, streams)
            P.barrier()
            phase_ffn1(l, streams)
            P.barrier()
            if done("f1_%d" % l):
                return
            phase_ffn2(l, streams, last)
            P.barrier()
            if done("f2_%d" % l):
                return

    run()
    P.barrier()
    P.emit()
    return nc


def make_in_maps(inputs):
    f = lambda a: np.ascontiguousarray(np.asarray(a, dtype=np.float32))
    x = f(inputs["x"])
    c = f(inputs["c"])
    ctx = f(inputs["ctx"])
    c_ctx = f(inputs["c_ctx"])
    shared = {}
    for k in ("ada_w", "ada_b", "norm1_w", "w_in", "sgu_ln_w", "sgu_ln_b", "hgrn_lower_bounds", "hgrn_norm_w",
              "w_branch_a", "w_branch_b", "w_out", "norm2_w", "ffn_w_up", "ffn_w_down"):
        shared[k] = f(inputs[k])
    shared["sgu_wT"] = f(np.transpose(f(inputs["sgu_w"]), (0, 3, 1, 2)))
    shared["sgu_b"] = f(f(inputs["sgu_b"]).reshape(2, D))
    cw = f(inputs["ffn_conv_w"]).reshape(2, 9, NCC, 128)
    shared["convw"] = f(np.transpose(cw, (0, 3, 2, 1)))
    shared["convb"] = f(np.transpose(f(inputs["ffn_conv_b"]).reshape(2, NCC, 128), (0, 2, 1)))
    shared["final_norm_w"] = f(f(inputs["final_norm_w"]).reshape(1, D))
    shared["consts"] = make_consts()
    maps = []
    for core in range(8):
        b = core % 4
        m = dict(shared)
        m["x"] = f(x[b])
        m["ctx"] = f(ctx[b])
        cc = np.stack([c[b].reshape(8, 128).T, c_ctx.reshape(8, 128).T], axis=-1)
        m["cc"] = f(cc)
        maps.append(m)
    return maps


_NC_CACHE = {}


def kernel(**inputs):
    if "nc" not in _NC_CACHE:
        _NC_CACHE["nc"] = build()
    nc = _NC_CACHE["nc"]
    maps = make_in_maps(inputs)
    res = run_bass_kernel_spmd(nc, maps, core_ids=list(range(8)))
    outs = [np.asarray(res.results[b]["out"], dtype=np.float32) for b in range(4)]
    return np.stack(outs, axis=0)
```
